# Optimizing a Trainium2 kernel written in Bass

```python
import jax, jax.numpy as jnp
from jax import lax
import numpy as np

D_MODEL = 1024
BATCH = 2
SEQ = 8192
DEPTH = 4

HEAD_DIM = 64
NSA_HEADS = 8
NSA_KV_GROUPS = 2
NSA_HPG = NSA_HEADS // NSA_KV_GROUPS
MOBA_HEADS = 8
CMP_STRIDE = 16
CMP_LEN = 2 * CMP_STRIDE
CMP_HIDDEN = 128
SEL_BLOCK = 64
SEL_TOPK = 16
WINDOW = 512
MOBA_BLOCK = 256
MOBA_TOPK = 3
Q_CHUNK = 128
ROPE_THETA = 10000.0
D_FF = -(-8 * D_MODEL // (3 * 256)) * 256
NEG = -1e30
FORCE_SCORE = 1e4

NSA_QW = NSA_HEADS * HEAD_DIM
NSA_KVW = NSA_KV_GROUPS * HEAD_DIM
NSA_GATEW = NSA_HEADS * 3
MOBA_W = MOBA_HEADS * HEAD_DIM
BRANCH_GATEW = 2 * D_MODEL
IN_SPLITS = (NSA_QW, NSA_KVW, NSA_KVW, NSA_KVW, NSA_KVW, NSA_KVW, NSA_KVW, NSA_GATEW, MOBA_W, MOBA_W, MOBA_W, BRANCH_GATEW)
IN_COLS = NSA_QW + 6 * NSA_KVW + NSA_GATEW + 3 * MOBA_W + BRANCH_GATEW

kernel_name = "nsa_moba_gated_hybrid_trunk"


def rms_norm(x, g, eps=1e-6):
    x32 = x.astype(jnp.float32)
    y = x32 * lax.rsqrt(jnp.mean(x32 * x32, axis=-1, keepdims=True) + eps)
    return y.astype(x.dtype) * g


def rope_angles(pos):
    inv = 1.0 / (ROPE_THETA ** (jnp.arange(0, HEAD_DIM, 2, dtype=jnp.float32) / HEAD_DIM))
    ang = pos.astype(jnp.float32)[..., None] * inv
    return jnp.cos(ang), jnp.sin(ang)


def apply_rope(x, cos, sin):
    cos = cos.astype(x.dtype)
    sin = sin.astype(x.dtype)
    x1, x2 = jnp.split(x, 2, axis=-1)
    return jnp.concatenate([x1 * cos - x2 * sin, x2 * cos + x1 * sin], axis=-1)


def masked_softmax(s, mask):
    s = jnp.where(mask, s.astype(jnp.float32), NEG)
    return jnp.where(mask, jax.nn.softmax(s, axis=-1), 0.0)


def _heads(t, n):
    b, s, _ = t.shape
    return t.reshape(b, s, n, HEAD_DIM).transpose(0, 2, 1, 3)


def nsa_mixer(q, k_cmp, v_cmp, k_slc, v_slc, k_win, v_win, gate_logits, positions, cos, sin,
              q_gain, k_gain, cmp_pe, ck_w1, ck_w2, cv_w1, cv_w2):
    B, S, _ = q.shape
    G, HPG, DH = NSA_KV_GROUPS, NSA_HPG, HEAD_DIM
    scale = DH ** -0.5
    cs, sn = cos[:, None], sin[:, None]
    q = apply_rope(rms_norm(_heads(q, NSA_HEADS), q_gain), cs, sn).reshape(B, G, HPG, S, DH)
    k_slc = apply_rope(rms_norm(_heads(k_slc, G), k_gain), cs, sn)
    k_win = apply_rope(rms_norm(_heads(k_win, G), k_gain), cs, sn)
    v_slc = _heads(v_slc, G)
    v_win = _heads(v_win, G)

    n_cmp = S // CMP_STRIDE - 1

    def blocks(t):
        tc = t.reshape(B, G, S // CMP_STRIDE, CMP_STRIDE, DH)
        blk = jnp.concatenate([tc[:, :, :-1], tc[:, :, 1:]], axis=3) + cmp_pe
        return blk.reshape(B, G, n_cmp, CMP_LEN * DH)

    k_c = jax.nn.gelu(blocks(_heads(k_cmp, G)) @ ck_w1) @ ck_w2
    v_c = jax.nn.gelu(blocks(_heads(v_cmp, G)) @ cv_w1) @ cv_w2
    cmp_start = jnp.arange(n_cmp) * CMP_STRIDE
    cmp_end = cmp_start + CMP_LEN - 1
    cos_c, sin_c = rope_angles(positions[:, cmp_end])
    k_c = apply_rope(rms_norm(k_c, k_gain), cos_c[:, None], sin_c[:, None])

    n_sb = S // SEL_BLOCK
    n_pick = min(SEL_TOPK, n_sb)
    sel_start = jnp.arange(n_sb) * SEL_BLOCK
    overlap = ((cmp_start[:, None] < sel_start[None, :] + SEL_BLOCK) &
               (cmp_start[:, None] + CMP_LEN > sel_start[None, :])).astype(jnp.float32)
    k_sb = k_slc.reshape(B, G, n_sb, SEL_BLOCK, DH)
    v_sb = v_slc.reshape(B, G, n_sb, SEL_BLOCK, DH)
    k_wp = jnp.pad(k_win, ((0, 0), (0, 0), (WINDOW, 0), (0, 0)))
    v_wp = jnp.pad(v_win, ((0, 0), (0, 0), (WINDOW, 0), (0, 0)))
    gates = jax.nn.sigmoid(gate_logits).reshape(B, S, G, HPG, 3).transpose(0, 2, 3, 1, 4)
    bi = jnp.arange(B)[:, None, None, None]
    gi = jnp.arange(G)[None, :, None, None]
    blk_id = jnp.arange(n_sb)

    def chunk(ci):
        s0 = ci * Q_CHUNK
        t = s0 + jnp.arange(Q_CHUNK)
        qc = lax.dynamic_slice_in_dim(q, s0, Q_CHUNK, axis=3)
        s_c = jnp.einsum('bghqd,bgnd->bghqn', qc, k_c) * scale
        p_c = masked_softmax(s_c, cmp_end[None, :] <= t[:, None])
        o_c = jnp.einsum('bghqn,bgnd->bghqd', p_c.astype(v_c.dtype), v_c)
        imp = jnp.einsum('bghqn,nj->bgqj', p_c, overlap)
        forced = (blk_id[None, :] == 0) | (blk_id[None, :] == (t // SEL_BLOCK)[:, None])
        imp = jnp.where(forced, FORCE_SCORE, imp)
        imp = jnp.where(sel_start[None, :] <= t[:, None], imp, NEG)
        _, idx = lax.top_k(imp, n_pick)
        kg = k_sb[bi, gi, idx]
        vg = v_sb[bi, gi, idx].reshape(B, G, Q_CHUNK, n_pick * SEL_BLOCK, DH)
        tok = idx[..., None] * SEL_BLOCK + jnp.arange(SEL_BLOCK)
        m_s = (tok <= t[:, None, None]).reshape(B, G, 1, Q_CHUNK, n_pick * SEL_BLOCK)
        s_s = jnp.einsum('bghqd,bgqnkd->bghqnk', qc, kg).reshape(B, G, HPG, Q_CHUNK, n_pick * SEL_BLOCK) * scale
        p_s = masked_softmax(s_s, m_s)
        o_s = jnp.einsum('bghqm,bgqmd->bghqd', p_s.astype(vg.dtype), vg)
        kw = lax.dynamic_slice_in_dim(k_wp, s0, Q_CHUNK + WINDOW, axis=2)
        vw = lax.dynamic_slice_in_dim(v_wp, s0, Q_CHUNK + WINDOW, axis=2)
        kpos = s0 - WINDOW + jnp.arange(Q_CHUNK + WINDOW)
        m_w = (kpos[None, :] <= t[:, None]) & (kpos[None, :] > t[:, None] - WINDOW) & (kpos[None, :] >= 0)
        s_w = jnp.einsum('bghqd,bgkd->bghqk', qc, kw) * scale
        p_w = masked_softmax(s_w, m_w)
        o_w = jnp.einsum('bghqk,bgkd->bghqd', p_w.astype(vw.dtype), vw)
        gc = lax.dynamic_slice_in_dim(gates, s0, Q_CHUNK, axis=3)
        return gc[..., 0:1] * o_c + gc[..., 1:2] * o_s + gc[..., 2:3] * o_w

    out = lax.map(chunk, jnp.arange(S // Q_CHUNK))
    return out.transpose(1, 0, 4, 2, 3, 5).reshape(B, S, NSA_HEADS * DH)


def moba_mixer(q, k, v, cos, sin, q_gain, k_gain):
    B, S, _ = q.shape
    H, DH, MB = MOBA_HEADS, HEAD_DIM, MOBA_BLOCK
    scale = DH ** -0.5
    cs, sn = cos[:, None], sin[:, None]
    q = apply_rope(rms_norm(_heads(q, H), q_gain), cs, sn)
    k = apply_rope(rms_norm(_heads(k, H), k_gain), cs, sn)
    v = _heads(v, H)
    n_blk = -(-S // MB)
    pad = n_blk * MB - S
    k_p = jnp.pad(k, ((0, 0), (0, 0), (0, pad), (0, 0)))
    v_p = jnp.pad(v, ((0, 0), (0, 0), (0, pad), (0, 0)))
    k_blk = k_p.reshape(B, H, n_blk, MB, DH)
    v_blk = v_p.reshape(B, H, n_blk, MB, DH)
    k_mean = jnp.mean(k_blk.astype(jnp.float32), axis=3).astype(k.dtype)
    n_pick = min(MOBA_TOPK, n_blk)
    bi = jnp.arange(B)[:, None, None, None]
    hi = jnp.arange(H)[None, :, None, None]
    blk_id = jnp.arange(n_blk)

    def chunk(ci):
        s0 = ci * Q_CHUNK
        t = s0 + jnp.arange(Q_CHUNK)
        own = s0 // MB
        qc = lax.dynamic_slice_in_dim(q, s0, Q_CHUNK, axis=2)
        gate = jnp.einsum('bhqd,bhnd->bhqn', qc, k_mean).astype(jnp.float32)
        _, idx = lax.top_k(jnp.where(blk_id < own, gate, NEG), n_pick)
        m_sel = jnp.repeat(idx < own, MB, axis=-1)
        kg = k_blk[bi, hi, idx]
        vg = v_blk[bi, hi, idx].reshape(B, H, Q_CHUNK, n_pick * MB, DH)
        s_sel = jnp.einsum('bhqd,bhqnkd->bhqnk', qc, kg).reshape(B, H, Q_CHUNK, n_pick * MB) * scale
        ko = lax.dynamic_slice_in_dim(k_p, own * MB, MB, axis=2)
        vo = lax.dynamic_slice_in_dim(v_p, own * MB, MB, axis=2)
        s_own = jnp.einsum('bhqd,bhkd->bhqk', qc, ko) * scale
        m_own = jnp.broadcast_to((own * MB + jnp.arange(MB))[None, :] <= t[:, None], s_own.shape)
        p = masked_softmax(jnp.concatenate([s_sel, s_own], axis=-1),
                           jnp.concatenate([m_sel, m_own], axis=-1)).astype(v.dtype)
        return (jnp.einsum('bhqm,bhqmd->bhqd', p[..., :n_pick * MB], vg) +
                jnp.einsum('bhqk,bhkd->bhqd', p[..., n_pick * MB:], vo))

    out = lax.map(chunk, jnp.arange(S // Q_CHUNK))
    return out.transpose(1, 0, 3, 2, 4).reshape(B, S, H * DH)


def setup_inputs(seed: int = 0) -> dict:
    key = jax.random.key(seed)
    ks = jax.random.split(key, 24)

    def nrm(k, shape, s):
        return jax.random.normal(k, shape, jnp.float32) * s

    L, D, DH = DEPTH, D_MODEL, HEAD_DIM
    positions = (jnp.arange(SEQ, dtype=jnp.int32)[None, :] +
                 jax.random.randint(ks[2], (BATCH, 1), 0, 1024, dtype=jnp.int32)).astype(jnp.int32)
    return {
        "x": nrm(ks[0], (BATCH, SEQ, D), 1.0),
        "c": nrm(ks[1], (BATCH, D), 1.0),
        "positions": positions,
        "w_ada": nrm(ks[3], (L, D, 6 * D), 0.5 * D ** -0.5),
        "b_ada": nrm(ks[4], (L, 6 * D), 0.02),
        "norm_mix": 1.0 + nrm(ks[5], (L, D), 0.02),
        "norm_ffn": 1.0 + nrm(ks[6], (L, D), 0.02),
        "w_in": nrm(ks[7], (L, D, IN_COLS), D ** -0.5),
        "nsa_q_gain": 1.0 + nrm(ks[8], (L, DH), 0.02),
        "nsa_k_gain": 1.0 + nrm(ks[9], (L, DH), 0.02),
        "nsa_cmp_pe": nrm(ks[10], (L, CMP_LEN, DH), 0.1),
        "nsa_cmp_k_w1": nrm(ks[11], (L, CMP_LEN * DH, CMP_HIDDEN), (CMP_LEN * DH) ** -0.5),
        "nsa_cmp_k_w2": nrm(ks[12], (L, CMP_HIDDEN, DH), CMP_HIDDEN ** -0.5),
        "nsa_cmp_v_w1": nrm(ks[13], (L, CMP_LEN * DH, CMP_HIDDEN), (CMP_LEN * DH) ** -0.5),
        "nsa_cmp_v_w2": nrm(ks[14], (L, CMP_HIDDEN, DH), CMP_HIDDEN ** -0.5),
        "moba_q_gain": 1.0 + nrm(ks[15], (L, DH), 0.02),
        "moba_k_gain": 1.0 + nrm(ks[16], (L, DH), 0.02),
        "w_up_nsa": nrm(ks[17], (L, NSA_QW, D), NSA_QW ** -0.5),
        "w_up_moba": nrm(ks[18], (L, MOBA_W, D), MOBA_W ** -0.5),
        "w_out": nrm(ks[19], (L, D, D), D ** -0.5),
        "w_ffn_in": nrm(ks[20], (L, D, 2 * D_FF), D ** -0.5),
        "w_ffn_out": nrm(ks[21], (L, D_FF, D), D_FF ** -0.5),
    }


def reference(x, c, positions, w_ada, b_ada, norm_mix, norm_ffn, w_in,
              nsa_q_gain, nsa_k_gain, nsa_cmp_pe, nsa_cmp_k_w1, nsa_cmp_k_w2,
              nsa_cmp_v_w1, nsa_cmp_v_w2, moba_q_gain, moba_k_gain,
              w_up_nsa, w_up_moba, w_out, w_ffn_in, w_ffn_out):
    cos, sin = rope_angles(positions)
    split_at = np.cumsum(IN_SPLITS)[:-1].tolist()
    for l in range(DEPTH):
        ada = jax.nn.silu(c) @ w_ada[l] + b_ada[l]
        sh_m, sc_m, g_m, sh_f, sc_f, g_f = jnp.split(ada[:, None, :], 6, axis=-1)
        h = rms_norm(x, norm_mix[l]) * (1.0 + sc_m) + sh_m
        (q_a, kc_a, vc_a, ks_a, vs_a, kw_a, vw_a, g_a,
         q_b, k_b, v_b, g_br) = jnp.split(h @ w_in[l], split_at, axis=-1)
        y_a = nsa_mixer(q_a, kc_a, vc_a, ks_a, vs_a, kw_a, vw_a, g_a, positions, cos, sin,
                        nsa_q_gain[l], nsa_k_gain[l], nsa_cmp_pe[l], nsa_cmp_k_w1[l],
                        nsa_cmp_k_w2[l], nsa_cmp_v_w1[l], nsa_cmp_v_w2[l])
        y_b = moba_mixer(q_b, k_b, v_b, cos, sin, moba_q_gain[l], moba_k_gain[l])
        gate_a, gate_b = jnp.split(jax.nn.sigmoid(g_br), 2, axis=-1)
        merged = gate_a * (y_a @ w_up_nsa[l]) + gate_b * (y_b @ w_up_moba[l])
        x = x + g_m * (merged @ w_out[l])
        h = rms_norm(x, norm_ffn[l]) * (1.0 + sc_f) + sh_f
        gt, up = jnp.split(h @ w_ffn_in[l], 2, axis=-1)
        x = x + g_f * ((jax.nn.silu(gt) * up) @ w_ffn_out[l])
    return x
```

```python
import contextlib
import numpy as np
import ml_dtypes
import concourse.bass as bass
import concourse.mybir as mybir
from concourse.bass_utils import run_bass_kernel_spmd

F32 = mybir.dt.float32
BF16 = mybir.dt.bfloat16
I32 = mybir.dt.int32
AF = mybir.ActivationFunctionType
ALU = mybir.AluOpType
AX = mybir.AxisListType

D = 1024
SEQ = 8192
NBATCH = 2
DEPTH = 4
NT = 16
DFF = 2816
NIN = 4888
EPS = 1e-6
MASKV = -1.0e4
EPOCH = 10 ** 9
VW_ = 780

LAYER_W = {
    "w_ada": [D, 6 * D], "b_ada": [6 * D], "norm_mix": [D], "norm_ffn": [D], "w_in": [D, NIN],
    "nsa_q_gain": [64], "nsa_k_gain": [64], "nsa_cmp_pe": [32, 64],
    "nsa_cmp_k_w1": [2048, 128], "nsa_cmp_k_w2": [128, 64], "nsa_cmp_v_w1": [2048, 128], "nsa_cmp_v_w2": [128, 64],
    "moba_q_gain": [64], "moba_k_gain": [64], "w_up_nsa": [512, D], "w_up_moba": [512, D], "w_out": [D, D],
    "w_ffn_in": [D, 2 * DFF], "w_ffn_out": [DFF, D],
}


class Buf:
    __slots__ = ("name", "lw", "rd", "excl")

    def __init__(self, name, pend=None, excl=False):
        self.name = name
        self.lw = None
        self.rd = dict(pend) if pend else {}
        self.excl = excl


class Sched:
    ENG = ("pe", "dve", "act", "pool", "sp")

    def __init__(self, nc, n_dma_sems=48):
        self.nc = nc
        self.eng = {"pe": nc.tensor, "dve": nc.vector, "act": nc.scalar, "pool": nc.gpsimd, "sp": nc.sync}
        self.prog = {e: [] for e in self.ENG}
        self.cnt = {e: 0 for e in self.ENG}
        self.sems = {}
        self.waited = {e: {} for e in self.ENG}
        self.n_dma = n_dma_sems
        self.dma_val = [0] * n_dma_sems
        self.dma_next = 0
        self._stack = None
        self.n_inst = 0

    def attach(self, stack):
        self._stack = stack
        self.dma_sems = [stack.enter_context(self.nc.semaphore(f"dq{i}")) for i in range(self.n_dma)]

    def _sem(self, e, epoch):
        k = (e, epoch)
        if k not in self.sems:
            self.sems[k] = self._stack.enter_context(self.nc.semaphore(f"s_{e}_{epoch}"))
        return self.sems[k]

    def _wait(self, e, tok):
        if tok is None:
            return
        if tok[0] == "c":
            _, f, n = tok
            if f == e and e == "pe":
                return
            epoch, v = divmod(n - 1, EPOCH)
            w = self.waited[e]
            for ep2 in range(epoch + 1, self.cnt[f] // EPOCH + 2):
                if w.get(("c", f, ep2), 0) > 0:
                    return
            key = ("c", f, epoch)
            if w.get(key, 0) >= v + 1:
                return
            w[key] = v + 1
            sem = self._sem(f, epoch)
            eng = self.eng[e]
            self.prog[e].append(lambda eng=eng, sem=sem, v=v: eng.wait_ge(sem, v + 1))
        else:
            _, k, val = tok
            key = ("d", k)
            if self.waited[e].get(key, 0) >= val:
                return
            self.waited[e][key] = val
            sem = self.dma_sems[k]
            eng = self.eng[e]
            self.prog[e].append(lambda eng=eng, sem=sem, val=val: eng.wait_ge(sem, val))

    def _deps(self, e, reads, writes):
        for b in reads:
            self._wait(e, b.lw)
            if b.excl:
                for f, tok in list(b.rd.items()):
                    if f != e:
                        self._wait(e, tok)
        for b in writes:
            self._wait(e, b.lw)
            for tok in list(b.rd.values()):
                self._wait(e, tok)

    def op(self, e, fn, reads=(), writes=()):
        self._deps(e, reads, writes)
        self.cnt[e] += 1
        n = self.cnt[e]
        epoch, v = divmod(n - 1, EPOCH)
        sem = self._sem(e, epoch)
        eng = self.eng[e]
        name, a, k = fn
        self.prog[e].append(lambda eng=eng, sem=sem, name=name, a=a, k=k: getattr(eng, name)(*a, **k).then_inc(sem, 1))
        tok = ("c", e, n)
        for b in writes:
            b.lw = tok
            b.rd = {}
        for b in reads:
            if b not in writes:
                b.rd[e] = tok
        self.n_inst += 1
        return tok

    def dma(self, q, out, in_, reads=(), writes=(), **kw):
        self._deps(q, reads, writes)
        k = self.dma_next
        self.dma_next = (k + 1) % self.n_dma
        if self.dma_val[k] > 0:
            self._wait(q, ("d", k, self.dma_val[k]))
        self.dma_val[k] += 16
        val = self.dma_val[k]
        sem = self.dma_sems[k]
        eng = self.eng[q]
        self.prog[q].append(lambda eng=eng, sem=sem, out=out, in_=in_, kw=kw:
                            eng.dma_start(out=out, in_=in_, **kw).then_inc(sem, 16))
        tok = ("d", k, val)
        for b in writes:
            b.lw = tok
            b.rd = {}
        for b in reads:
            b.rd[("dq", k)] = tok
        self.n_inst += 1
        return tok

    def finish(self, e, bufs):
        for b in bufs:
            self._wait(e, b.lw)

    def emit(self, block):
        prog = self.prog

        @block.tensor
        def _(x):
            for f in prog["pe"]:
                f()

        @block.vector
        def _(x):
            for f in prog["dve"]:
                f()

        @block.scalar
        def _(x):
            for f in prog["act"]:
                f()

        @block.gpsimd
        def _(x):
            for f in prog["pool"]:
                f()

        @block.sync
        def _(x):
            for f in prog["sp"]:
                f()


def E(name, *a, **k):
    return (name, a, k)


def _merge_tok(pend, tok):
    if tok is None:
        return
    key = (tok[0], tok[1])
    old = pend.get(key)
    if old is None or old[2] < tok[2]:
        pend[key] = tok


class Arena:
    def __init__(self, tensor, ncols):
        self.t = tensor
        self.n = ncols
        self.top = 0
        self.live = []
        self.pend = {}

    def carve(self, name, ncols):
        ncols = (ncols + 1) // 2 * 2
        assert self.top + ncols <= self.n, f"arena overflow {name}: {self.top}+{ncols}>{self.n}"
        a = self.top
        self.top += ncols
        b = Buf(name, self.pend)
        self.live.append((a, self.top, b))
        return self.t[:, a:a + ncols], b

    def mark(self):
        return self.top

    def release(self, mark=0):
        keep = []
        for (a, e, b) in self.live:
            if a >= mark:
                _merge_tok(self.pend, b.lw)
                for tok in b.rd.values():
                    _merge_tok(self.pend, tok)
            else:
                keep.append((a, e, b))
        self.live = keep
        self.top = mark


def bc(ap, shape):
    return ap.to_broadcast(list(shape))


class Prog:
    def __init__(self, mode, layers):
        self.mode = mode
        self.layers = layers
        self.inputs = {}
        self.outputs = {}
        nc = self.nc = bass.Bass("TRN2", target_bir_lowering=False)
        self.S = Sched(nc)
        self._pv_pend = []

    def din(self, name, shape, dt=F32):
        self.inputs[name] = (list(shape), dt)
        return self.nc.dram_tensor(name, list(shape), dt, kind="ExternalInput").ap()

    def dout(self, name, shape, dt=F32):
        self.outputs[name] = (list(shape), dt)
        return self.nc.dram_tensor(name, list(shape), dt, kind="ExternalOutput").ap()

    def dint(self, name, shape, dt):
        return self.nc.dram_tensor(name, list(shape), dt, kind="Internal").ap()

    def sb(self, name, shape, dt):
        return self.st.enter_context(self.nc.sbuf_tensor(name, list(shape), dt))

    def ps(self, name, shape, dt):
        return self.st.enter_context(self.nc.psum_tensor(name, list(shape), dt))

    def w(self, name):
        if name not in self.wd:
            shp = LAYER_W[name]
            self.wd[name] = self.din(name, ([DEPTH] + shp) if self.mode == "F" else shp)
        ap = self.wd[name]
        if self.mode == "F":
            return ap[self.l]
        return ap

    def build(self):
        nc = self.nc
        S = self.S
        mode = self.mode
        with contextlib.ExitStack() as st:
            self.st = st
            S.attach(st)
            self.wd = {}
            VR = [4] if mode == "F" else []
            self.x_own = self.din("x_own", VR + [NT, 128, D])
            self.Bx_d = Buf("x_d")
            self.c_col = self.din("c_col", [128, 8])
            cst = {}
            self.vr = 0
            for k, shp, dt in [("invf", [128, 32], F32), ("pos_own", [128, NT], I32), ("pos_cmp", [128, 4], I32),
                               ("ovl", [128, 4, 128], F32), ("diag", [128, 4, 128], F32), ("win", [128, 8, 128], F32),
                               ("cma", [128, 4, 128], F32), ("cmb", [128, 128], F32), ("fwj", [128, 256], F32),
                               ("cwj", [128, 256], F32), ("cwm", [128, 64], F32), ("noj", [128, 64], F32)]:
                per_vr = k not in ("invf", "pos_cmp", "ovl")
                cst[k] = self.din("k_" + k, (VR if per_vr else []) + shp, dt)
            self.cst = cst
            if mode == "A":
                self.KT_own = self.dout("KT_own", [128, 8, 2048], BF16)
                self.V_own = self.dout("V_own", [2048, VW_], BF16)
                self.QA_d = self.dout("QA_d", [128, NT, 512], BF16)
                self.QB_d = self.dout("QB_d", [128, NT, 512], BF16)
                self.GA_d = self.dout("GA_d", [128, NT, 24], F32)
            elif mode == "B":
                self.KT_all = self.din("KT_all", [4, 128, 8, 2048], BF16)
                self.V_all = self.din("V_all", [4, 2048, VW_], BF16)
                self.QA_d = self.din("QA_d", [128, NT, 512], BF16)
                self.QB_d = self.din("QB_d", [128, NT, 512], BF16)
                self.GA_d = self.din("GA_d", [128, NT, 24], F32)
                self.x_out = self.dout("x_out", [NT, 128, D])
            else:
                self.KT_all = self.dint("KT_all", [4, 128, 8, 2048], BF16)
                self.V_all = self.dint("V_all", [4, 2048, VW_], BF16)
                self.QA_dF = self.dint("QA_d", [4, 128, NT, 512], BF16)
                self.QB_dF = self.dint("QB_d", [4, 128, NT, 512], BF16)
                self.GA_dF = self.dint("GA_d", [4, 128, NT, 24], F32)
                self.ROPE_d = self.dint("ROPE_d", [4, 2, 128, NT, 32], F32)
                self.x_out = self.dout("x_out", [4, NT, 128, D])
            self.B_KTown, self.B_Vown = Buf("KTown"), Buf("Vown")
            self.B_KTall, self.B_Vall = Buf("KTall"), Buf("Vall")
            self.B_QAd, self.B_QBd, self.B_GAd = Buf("QAd"), Buf("QBd"), Buf("GAd")
            self.B_KT4 = [Buf(f"KT{r}") for r in range(4)] if mode == "F" else [self.B_KTall] * 4
            self.B_V4 = [Buf(f"V{r}") for r in range(4)] if mode == "F" else [self.B_Vall] * 4
            self.B_QA4 = [Buf(f"QA{r}") for r in range(4)]
            self.B_QB4 = [Buf(f"QB{r}") for r in range(4)]
            self.B_GA4 = [Buf(f"GA{r}") for r in range(4)]
            self.B_XD = [Buf(f"XD{r}") for r in range(4)]
            self.B_ROPE4 = [Buf(f"ROPEd{r}") for r in range(4)]
            self.B_xout = Buf("xout")
            if mode in ("B", "F"):
                self.YT_d = self.dint("YT_d", [128, 8, 2048], BF16)
            self.B_YTd = [Buf(f"YTd{i}") for i in range(NT)]

            self.X = self.sb("X", [128, NT, D], F32)
            self.BX = [Buf(f"X{i}") for i in range(NT)]
            self.ADA = self.sb("ADA", [128, 3, D], F32)
            self.BADA = Buf("ADA")
            self.GA = self.sb("GA", [128, NT, 24], F32)
            self.BGA = Buf("GA")
            self.COS = self.sb("COS", [128, NT, 32], F32)
            self.SIN = self.sb("SIN", [128, NT, 32], F32)
            self.COSC = self.sb("COSC", [128, 4, 32], F32)
            self.SINC = self.sb("SINC", [128, 4, 32], F32)
            self.BROPE = Buf("rope")
            self.GAIN = self.sb("GAIN", [128, 4, 64], F32)
            self.BGAIN = Buf("gain")
            self.identf = self.sb("identf", [128, 128], F32)
            self.identb = self.sb("identb", [128, 128], BF16)
            self.BID = Buf("ident")
            self.DIAG = self.sb("DIAG", [128, 4, 128], BF16)
            self.WIN = self.sb("WIN", [128, 8, 128], BF16)
            self.CMA = self.sb("CMA", [128, 4, 128], BF16)
            self.CMB = self.sb("CMB", [128, 128], BF16)
            self.FWJ = self.sb("FWJ", [128, 256], F32)
            self.CWJ = self.sb("CWJ", [128, 256], F32)
            self.CWM = self.sb("CWM", [128, 64], F32)
            self.NOJ = self.sb("NOJ", [128, 64], F32)
            self.BCST = Buf("consts")
            self.CREP = self.sb("CREP", [128, 8, 128], BF16)
            self.BCREP = Buf("crep")
            self.KcT = self.sb("KcT", [128, 512], BF16)
            self.BKcT = Buf("KcT")
            self.OVV = self.sb("OVV", [128, 4, 2, 193], BF16)
            self.BOVV = Buf("OVV")
            self.NRSt = self.sb("NRS", [128, 5 * 512], F32)
            self.NRS = [self.NRSt[:, k * 512:(k + 1) * 512] for k in range(5)]
            self.BNRS = [Buf(f"NRS{k}") for k in range(5)]
            self.T1 = self.NRSt[:, 3 * 512:5 * 512]
            self.SM = self.sb("SM", [128, 512], F32)
            self.BSM = [Buf(f"sm{k}") for k in range(16)]
            self.sm_next = 0
            ARN = 49152
            self.ARt = self.sb("ARENA", [128, ARN], BF16)
            self.AR = Arena(self.ARt, ARN)
            self.PF = [self.ps(f"PF{k}", [128, 512], F32) for k in range(6)]
            self.BPF = [Buf(f"PF{k}", excl=True) for k in range(6)]
            self.PBT = [self.ps(f"PBT{k}", [128, 1024], BF16) for k in range(2)]
            self.BPBT = [Buf(f"PBT{k}", excl=True) for k in range(2)]
            self.pbt_i = 0

            block = st.enter_context(nc.Block())
            self.setup()
            import os
            self.dbg = int(os.environ.get("KDBG", "99"))
            if mode == "F":
                self.flow_f()
            for l in (self.layers if mode != "F" else []):
                self.l = l
                if mode in ("A", "F"):
                    if self.dbg >= 1:
                        self.ada(0)
                    if self.dbg >= 2:
                        self.phase_a()
                if mode == "F":
                    self.exchange()
                if mode in ("B", "F"):
                    if mode == "B":
                        self.ada(0)
                        S.dma("sp", self.GA[:], self.GA_d[:, :, :], reads=[self.B_GAd], writes=[self.BGA])
                    if self.dbg >= 2:
                        self.phase_b0()
                    if self.dbg >= 3:
                        self.phase_nsa()
                    if self.dbg >= 4:
                        self.phase_moba()
                    if self.dbg >= 5:
                        self.phase_b2()
                    if self.dbg >= 6:
                        self.ada(1)
                        self.phase_ffn()
            if mode == "F":
                S.finish("sp", self.B_XD)
            elif mode in ("B", "F"):
                for i in range(NT):
                    S.dma("sp", self.x_out[i, :, :], self.X[:, i, :], reads=[self.BX[i]], writes=[self.B_xout])
                if mode == "B" and self.dbg < 99:
                    ytdbg = self.dout("YT_dbg", [128, 8, 2048], BF16)
                    for cch in range(8):
                        S.dma("sp", ytdbg[:, cch, :], self.YT_d[:, cch, :], reads=self.B_YTd, writes=[self.B_xout])
                S.finish("sp", [self.B_xout])
            else:
                S.finish("sp", [self.B_KTown, self.B_Vown, self.B_QAd, self.B_QBd, self.B_GAd])
            S.emit(block)
        return nc

    def set_vr(self, vr):
        self.vr = vr
        self.QA_d, self.QB_d, self.GA_d = self.QA_dF[vr], self.QB_dF[vr], self.GA_dF[vr]
        self.KT_own, self.V_own = self.KT_all[vr], self.V_all[vr]
        self.B_QAd, self.B_QBd, self.B_GAd = self.B_QA4[vr], self.B_QB4[vr], self.B_GA4[vr]
        self.B_KTown, self.B_Vown = self.B_KT4[vr], self.B_V4[vr]

    def load_x(self, vr, first):
        S = self.S
        src = self.x_own if first else self.x_out
        for q4 in range(4):
            S.dma("sp", self.X[:, 4 * q4:4 * q4 + 4, :], src[vr, 4 * q4:4 * q4 + 4, :, :].rearrange("i p d -> p i d"),
                  reads=[self.B_XD[vr]] if not first else [self.Bx_d], writes=self.BX[4 * q4:4 * q4 + 4])

    def store_x(self, vr):
        S = self.S
        for q4 in range(4):
            S.dma("sp", self.x_out[vr, 4 * q4:4 * q4 + 4, :, :].rearrange("i p d -> p i d"), self.X[:, 4 * q4:4 * q4 + 4, :],
                  reads=self.BX[4 * q4:4 * q4 + 4], writes=[self.B_XD[vr]])

    def load_vr_consts(self, vr):
        S = self.S
        cst = self.cst
        Bc = Buf("cdram")
        S.dma("pool", self.DIAG[:], cst["diag"][vr], reads=[Bc], writes=[self.BCST])
        S.dma("pool", self.WIN[:], cst["win"][vr], reads=[Bc], writes=[self.BCST])
        S.dma("pool", self.CMA[:], cst["cma"][vr], reads=[Bc], writes=[self.BCST])
        S.dma("pool", self.CMB[:], cst["cmb"][vr], reads=[Bc], writes=[self.BCST])
        S.dma("sp", self.FWJ[:], cst["fwj"][vr], reads=[Bc], writes=[self.BCST])
        S.dma("sp", self.CWJ[:], cst["cwj"][vr], reads=[Bc], writes=[self.BCST])
        S.dma("sp", self.CWM[:], cst["cwm"][vr], reads=[Bc], writes=[self.BCST])
        S.dma("sp", self.NOJ[:], cst["noj"][vr], reads=[Bc], writes=[self.BCST])

    def flow_f(self):
        S = self.S
        for l in self.layers:
            self.l = l
            first = (l == self.layers[0])
            self.ada(0)
            for vr in range(4):
                self.set_vr(vr)
                self.load_x(vr, first)
                S.dma("sp", self.COS[:], self.ROPE_d[vr, 0], reads=[self.B_ROPE4[vr]], writes=[self.BROPE])
                S.dma("sp", self.SIN[:], self.ROPE_d[vr, 1], reads=[self.B_ROPE4[vr]], writes=[self.BROPE])
                self.phase_a()
                S.dma("sp", self.GA_d[:, :, :], self.GA[:], reads=[self.BGA], writes=[self.B_GAd])
            self.phase_b0()
            for vr in range(4):
                self.set_vr(vr)
                self.load_vr_consts(vr)
                self.load_x(vr, first)
                S.dma("sp", self.GA[:], self.GA_d[:, :, :], reads=[self.B_GAd], writes=[self.BGA])
                self.phase_nsa()
                self.phase_moba()
                self.phase_b2()
                self.store_x(vr)
            self.ada(1)
            for vr in range(4):
                self.set_vr(vr)
                self.load_x(vr, False)
                self.phase_ffn()
                self.store_x(vr)

    def sm(self, n, name="sm"):
        assert n <= 32
        k = self.sm_next
        self.sm_next = (k + 1) % 16
        return self.SM[:, k * 32:k * 32 + n], self.BSM[k]

    def next_pbt(self):
        k = self.pbt_i
        self.pbt_i ^= 1
        return self.PBT[k], self.BPBT[k]

    def setup(self):
        S = self.S
        cst = self.cst
        if self.mode != "F":
            for i in range(NT):
                S.dma("sp", self.X[:, i, :], self.x_own[i, :, :], reads=[self.Bx_d], writes=[self.BX[i]])
        S.op("pool", E("memset", self.identf[:], 0.0), writes=[self.BID])
        S.op("pool", E("affine_select", out=self.identf[:], in_=self.identf[:], pattern=[[-1, 128]],
                                               compare_op=ALU.not_equal, fill=1.0, base=0, channel_multiplier=1),
             reads=[self.BID], writes=[self.BID])
        S.op("dve", E("tensor_copy", out=self.identb[:], in_=self.identf[:]), reads=[self.BID], writes=[self.BID])
        Bc = Buf("cdram")
        if self.mode != "F":
            S.dma("pool", self.DIAG[:], cst["diag"][:, :, :], reads=[Bc], writes=[self.BCST])
            S.dma("pool", self.WIN[:], cst["win"][:, :, :], reads=[Bc], writes=[self.BCST])
            S.dma("pool", self.CMA[:], cst["cma"][:, :, :], reads=[Bc], writes=[self.BCST])
            S.dma("pool", self.CMB[:], cst["cmb"][:, :], reads=[Bc], writes=[self.BCST])
            S.dma("sp", self.FWJ[:], cst["fwj"][:, :], reads=[Bc], writes=[self.BCST])
            S.dma("sp", self.CWJ[:], cst["cwj"][:, :], reads=[Bc], writes=[self.BCST])
            S.dma("sp", self.CWM[:], cst["cwm"][:, :], reads=[Bc], writes=[self.BCST])
            S.dma("sp", self.NOJ[:], cst["noj"][:, :], reads=[Bc], writes=[self.BCST])
        for nt in range(4):
            for g in range(2):
                S.dma("pool", self.OVV[:, nt, g, 0:128], cst["ovl"][:, nt, :], reads=[Bc], writes=[self.BOVV])
        S.op("pool", E("memset", self.OVV[:, :, :, 192:193], 1.0), reads=[], writes=[self.BOVV])
        cs, bcs = self.sm(8)
        S.dma("sp", cs, self.c_col[:, :], reads=[Bc], writes=[bcs])
        cs2, bcs2 = self.sm(8)
        S.op("act", E("activation", out=cs2, in_=cs, func=AF.Silu), reads=[bcs], writes=[bcs2])
        S.op("dve", E("tensor_copy", out=self.CREP[:], in_=bc(cs2.unsqueeze(2), [128, 8, 128])),
             reads=[bcs2], writes=[self.BCREP])
        invf = self.NRS[0][:, 0:32]
        S.dma("sp", invf, cst["invf"][:, :], reads=[Bc], writes=[self.BNRS[0]])
        posi = self.sb("posi", [128, NT + 4], I32)
        Bpi = Buf("posi")
        nvr = 4 if self.mode == "F" else 1
        jobs = [(vr, 0, NT, self.COS, self.SIN) for vr in range(nvr)] + [(-1, NT, NT + 4, self.COSC, self.SINC)]
        for (vr, c0, c1, cos_t, sin_t) in jobs:
            if vr >= 0:
                src = cst["pos_own"][vr] if self.mode == "F" else cst["pos_own"][:, :]
                S.dma("sp", posi[:, 0:NT], src, reads=[Bc], writes=[Bpi])
            else:
                S.dma("sp", posi[:, NT:NT + 4], cst["pos_cmp"][:, :], reads=[Bc], writes=[Bpi])
            posf, bpf = self.sm(NT + 4)
            S.op("dve", E("tensor_copy", out=posf, in_=posi[:]), reads=[Bpi], writes=[bpf])
            n = c1 - c0
            for s0 in range(0, n, 8):
                m = min(8, n - s0)
                A = self.NRS[1][:, 0:m * 32].rearrange("p (a b) -> p a b", a=m)
                Bq = self.NRS[2][:, 0:m * 32].rearrange("p (a b) -> p a b", a=m)
                Cq = self.NRS[3][:, 0:m * 32].rearrange("p (a b) -> p a b", a=m)
                pa = posf[:, c0 + s0:c0 + s0 + m]
                S.op("dve", E("tensor_tensor",
                    out=A, in0=bc(pa.unsqueeze(2), [128, m, 32]), in1=bc(invf.unsqueeze(1), [128, m, 32]), op=ALU.mult),
                    reads=[bpf, self.BNRS[0]], writes=[self.BNRS[1]])
                S.op("dve", E("tensor_scalar", out=Bq, in0=A, scalar1=float(1.0 / (2 * np.pi)),
                                                                  scalar2=12582912.0, op0=ALU.mult, op1=ALU.add),
                     reads=[self.BNRS[1]], writes=[self.BNRS[2]])
                S.op("dve", E("tensor_scalar", out=Bq, in0=Bq, scalar1=-12582912.0, scalar2=None,
                                                             op0=ALU.add), reads=[self.BNRS[2]], writes=[self.BNRS[2]])
                C1 = 6.28125
                C2 = float(2 * np.pi - 6.28125)
                S.op("dve", E("scalar_tensor_tensor", out=Cq, in0=Bq, scalar=-C1, in1=A,
                                                                                 op0=ALU.mult, op1=ALU.add),
                     reads=[self.BNRS[1], self.BNRS[2]], writes=[self.BNRS[3]])
                S.op("dve", E("scalar_tensor_tensor", out=A, in0=Bq, scalar=-C2, in1=Cq,
                                                                                 op0=ALU.mult, op1=ALU.add),
                     reads=[self.BNRS[2], self.BNRS[3]], writes=[self.BNRS[1]])
                S.op("dve", E("tensor_scalar", out=A, in0=A, scalar1=3.1415925, scalar2=-3.1415925,
                                                           op0=ALU.min, op1=ALU.max),
                     reads=[self.BNRS[1]], writes=[self.BNRS[1]])
                S.op("act", E("activation", out=sin_t[:, s0:s0 + m, :], in_=A, func=AF.Sin),
                     reads=[self.BNRS[1]], writes=[self.BROPE])
                S.op("act", E("activation", out=Cq, in_=A, func=AF.Abs), reads=[self.BNRS[1]], writes=[self.BNRS[3]])
                S.op("dve", E("tensor_scalar", out=Cq, in0=Cq, scalar1=-1.0, scalar2=float(np.pi / 2),
                              op0=ALU.mult, op1=ALU.add), reads=[self.BNRS[3]], writes=[self.BNRS[3]])
                S.op("act", E("activation", out=cos_t[:, s0:s0 + m, :], in_=Cq, func=AF.Sin),
                     reads=[self.BNRS[3]], writes=[self.BROPE])
            if self.mode == "F" and vr >= 0:
                S.dma("sp", self.ROPE_d[vr, 0], self.COS[:], reads=[self.BROPE], writes=[self.B_ROPE4[vr]])
                S.dma("sp", self.ROPE_d[vr, 1], self.SIN[:], reads=[self.BROPE], writes=[self.B_ROPE4[vr]])

    def ada(self, hf):
        S = self.S
        AR = self.AR
        mk = AR.mark()
        WA = [AR.carve(f"WA{k}", 8 * 512) for k in range(2)]
        w_ada = self.w("w_ada")
        b_ada = self.w("b_ada")
        Bw = Buf("wada_d")
        for cc in range(6):
            col0 = hf * 3072 + cc * 512
            wa, bwa = WA[cc % 2]
            wa3 = wa.rearrange("p (k n) -> p k n", k=8)
            S.dma("pool", wa3, w_ada[:, col0:col0 + 512].rearrange("(k p) n -> p k n", p=128), reads=[Bw], writes=[bwa])
            ba = self.NRS[cc % 2]
            bba = self.BNRS[cc % 2]
            S.dma("sp", ba[:, :], b_ada[col0:col0 + 512].partition_broadcast(128), reads=[Bw], writes=[bba])
            pf, bpf = self.PF[cc % 2], self.BPF[cc % 2]
            for kc in range(8):
                S.op("pe", E("matmul", pf[:, :], lhsT=self.CREP[:, kc, :], rhs=wa3[:, kc, :],
                                                                     start=(kc == 0), stop=(kc == 7)),
                     reads=[self.BCREP, bwa], writes=[bpf])
            dst = self.ADA[:, cc // 2, (cc % 2) * 512:(cc % 2) * 512 + 512]
            S.op("dve", E("tensor_tensor", out=dst, in0=pf[:, :], in1=ba[:, :], op=ALU.add),
                 reads=[bpf, bba], writes=[self.BADA])
        nv = self.w("norm_mix" if hf == 0 else "norm_ffn")
        S.dma("sp", self.T1[:, :], nv.partition_broadcast(128), reads=[Bw], writes=[self.BNRS[3], self.BNRS[4]])
        S.op("dve", E("scalar_tensor_tensor", out=self.ADA[:, 1, :], in0=self.ADA[:, 1, :], scalar=1.0,
                                                     in1=self.T1[:, :], op0=ALU.add, op1=ALU.mult),
             reads=[self.BADA, self.BNRS[3], self.BNRS[4]], writes=[self.BADA])
        AR.release(mk)

    def norm_T(self, i, dst, bdst, junk, bjunk, hb, bhb):
        S = self.S
        ss, bss = self.sm(1)
        rs, brs = self.sm(1)
        rstd, brstd = self.sm(1)
        S.op("act", E("activation", out=junk, in_=self.X[:, i, :], func=AF.Square, accum_out=ss),
             reads=[self.BX[i]], writes=[bjunk, bss])
        S.op("act", E("activation", out=rs, in_=ss, func=AF.Sqrt, scale=1.0 / D, bias=EPS),
             reads=[bss], writes=[brs])
        S.op("dve", E("reciprocal", out=rstd, in_=rs), reads=[brs], writes=[brstd])
        S.op("dve", E("scalar_tensor_tensor", out=self.T1[:, :], in0=self.X[:, i, :], scalar=rstd,
                                                     in1=self.ADA[:, 1, :], op0=ALU.mult, op1=ALU.mult),
             reads=[self.BX[i], brstd, self.BADA], writes=[self.BNRS[3], self.BNRS[4]])
        S.op("pool", E("tensor_tensor", out=hb, in0=self.T1[:, :], in1=self.ADA[:, 0, :], op=ALU.add),
             reads=[self.BNRS[3], self.BNRS[4], self.BADA], writes=[bhb])
        pbt, bpbt = self.next_pbt()
        for kc in range(8):
            S.op("pe", E("transpose", out=pbt[:, kc * 128:(kc + 1) * 128],
                                                            in_=hb[:, kc * 128:(kc + 1) * 128], identity=self.identb[:]),
                 reads=[bhb, self.BID], writes=[bpbt])
        S.op("act", E("activation", out=dst, in_=pbt[:, :].rearrange("p (k t) -> p k t", k=8), func=AF.Copy),
             reads=[bpbt], writes=[bdst])

    def nr_post(self, src, bsrc, nh, gain_kind, cos_ap, sin_ap, out, bout):
        S = self.S
        n = nh * 64
        SQ, YN, YG, RA, RB = [t[:, 0:n].rearrange("p (h d) -> p h d", h=nh) for t in self.NRS[0:3]] + \
                             [t[:, 0:nh * 32].rearrange("p (h d) -> p h d", h=nh) for t in self.NRS[3:5]]
        RA2 = self.NRS[3][:, 256:256 + nh * 32].rearrange("p (h d) -> p h d", h=nh)
        RB2 = self.NRS[4][:, 256:256 + nh * 32].rearrange("p (h d) -> p h d", h=nh)
        B0, B1, B2, B3, B4 = self.BNRS
        ssn, bssn = self.sm(nh)
        rsn, brsn = self.sm(nh)
        rstd, brstd = self.sm(nh)
        S.op("act", E("activation", out=SQ, in_=src, func=AF.Square), reads=[bsrc], writes=[B0])
        S.op("dve", E("tensor_reduce", out=ssn, in_=SQ, axis=AX.X, op=ALU.add), reads=[B0], writes=[bssn])
        S.op("act", E("activation", out=rsn, in_=ssn, func=AF.Sqrt, scale=1.0 / 64, bias=EPS),
             reads=[bssn], writes=[brsn])
        S.op("dve", E("reciprocal", out=rstd, in_=rsn), reads=[brsn], writes=[brstd])
        S.op("dve", E("tensor_tensor", out=YN, in0=src, in1=bc(rstd.unsqueeze(2), [128, nh, 64]), op=ALU.mult),
             reads=[bsrc, brstd], writes=[B1])
        S.op("pool", E("tensor_tensor", out=YG, in0=YN, in1=bc(self.GAIN[:, gain_kind, :].unsqueeze(1), [128, nh, 64]),
                                               op=ALU.mult), reads=[B1, self.BGAIN], writes=[B2])
        x1 = YG[:, :, 0:32]
        x2 = YG[:, :, 32:64]
        S.op("dve", E("tensor_tensor", out=RA, in0=x1, in1=cos_ap, op=ALU.mult), reads=[B2, self.BROPE], writes=[B3])
        S.op("pool", E("tensor_tensor", out=RB, in0=x2, in1=sin_ap, op=ALU.mult), reads=[B2, self.BROPE], writes=[B4])
        S.op("dve", E("tensor_tensor", out=out[:, :, 0:32], in0=RA, in1=RB, op=ALU.subtract),
             reads=[B3, B4], writes=[bout])
        S.op("pool", E("tensor_tensor", out=RA2, in0=x2, in1=cos_ap, op=ALU.mult), reads=[B2, self.BROPE], writes=[B3])
        S.op("dve", E("tensor_tensor", out=RB2, in0=x1, in1=sin_ap, op=ALU.mult), reads=[B2, self.BROPE], writes=[B4])
        S.op("pool", E("tensor_tensor", out=out[:, :, 32:64], in0=RA2, in1=RB2, op=ALU.add),
             reads=[B3, B4], writes=[bout])

    def load_gains(self):
        S = self.S
        Bg = Buf("gain_d")
        for k, nm in enumerate(["nsa_q_gain", "moba_q_gain", "nsa_k_gain", "moba_k_gain"]):
            S.dma("sp", self.GAIN[:, k, :], self.w(nm).partition_broadcast(128), reads=[Bg], writes=[self.BGAIN])
        S.op("dve", E("tensor_scalar", out=self.GAIN[:, 0:2, :], in0=self.GAIN[:, 0:2, :], scalar1=0.125,
                                              scalar2=None, op0=ALU.mult), reads=[self.BGAIN], writes=[self.BGAIN])

    def phase_a(self):
        S = self.S
        AR = self.AR
        mk = AR.mark()
        self.load_gains()
        hT, bhT_ = AR.carve("hT_all", 8 * 2048)
        hT3 = hT.rearrange("p (k t) -> p k t", k=8)
        BhT = [Buf(f"hT{i}", AR.pend) for i in range(NT)]
        junk, bjunk = AR.carve("junk", 1024)
        hb, bhb = AR.carve("hb", 1024)
        for i in range(NT):
            self.norm_T(i, hT3[:, :, i * 128:(i + 1) * 128], BhT[i], junk, bjunk, hb, bhb)
        if self.dbg == 2:
            AR.release(mk)
            return
        Wc = [AR.carve(f"Wc{k}", 8 * 512) for k in range(2)]
        NRb = [AR.carve(f"NRb{k}", 512) for k in range(2)]
        STG = [AR.carve(f"STG{k}", 512) for k in range(2)]
        STGV = [AR.carve(f"STGV{k}", 8 * 65) for k in range(2)]
        for k in range(2):
            S.op("pool", E("memset", STGV[k][0], 1.0), writes=[STGV[k][1]])
        w_in = self.w("w_in")
        Bw = Buf("win_d")
        chunks = [
            ("QA", None, 512), ("QB", [(1304, 512)], 512), ("KB", [(1816, 512)], 512),
            ("KSKW", [(768, 128), (1024, 128)], 256), ("RAW1", [(512, 256), (896, 128), (1152, 128)], 512),
            ("VB", [(2328, 512)], 512), ("GA", [(1280, 24)], 24),
        ]
        it = 0
        for ci, (name, srcs, ncol) in enumerate(chunks):
            if self.dbg < 10 and ci > self.dbg - 3:
                break
            wc, bwc = Wc[ci % 2]
            wc3 = wc.rearrange("p (k n) -> p k n", k=8)
            if name == "QA":
                wc5 = wc.rearrange("p (k hh g d) -> p k hh g d", k=8, hh=4, g=2)
                for g in range(2):
                    for hh in range(4):
                        S.dma("pool", wc5[:, :, hh, g, :],
                              w_in[:, (g * 4 + hh) * 64:(g * 4 + hh + 1) * 64].rearrange("(k p) d -> p k d", p=128),
                              reads=[Bw], writes=[bwc])
            else:
                c = 0
                for (s0, sn) in srcs:
                    S.dma("pool", wc3[:, :, c:c + sn], w_in[:, s0:s0 + sn].rearrange("(k p) n -> p k n", p=128),
                          reads=[Bw], writes=[bwc])
                    c += sn
            for i in range(NT):
                pf, bpf = self.PF[it % 2], self.BPF[it % 2]
                it += 1
                for kc in range(8):
                    S.op("pe", E("matmul",
                        pf[:, 0:ncol], lhsT=hT3[:, kc, i * 128:(i + 1) * 128], rhs=wc3[:, kc, 0:ncol],
                        start=(kc == 0), stop=(kc == 7)), reads=[BhT[i], bwc], writes=[bpf])
                nrb, bnrb = NRb[i % 2]
                stg, bstg = STG[i % 2]
                stgv, bstgv = STGV[i % 2]
                cosb = lambda nh: bc(self.COS[:, i, :].unsqueeze(1), [128, nh, 32])
                sinb = lambda nh: bc(self.SIN[:, i, :].unsqueeze(1), [128, nh, 32])
                if name in ("QA", "QB", "KB", "KSKW"):
                    nh = ncol // 64
                    kind = {"QA": 0, "QB": 1, "KB": 3, "KSKW": 2}[name]
                    self.nr_post(pf[:, 0:ncol].rearrange("p (h d) -> p h d", h=nh), bpf, nh, kind, cosb(nh), sinb(nh),
                                 nrb[:, 0:ncol].rearrange("p (h d) -> p h d", h=nh), bnrb)
                    nblk = ncol // 128
                    pbt, bpbt = self.next_pbt()
                    for k in range(nblk):
                        S.op("pe", E("transpose",
                            out=pbt[:, k * 128:(k + 1) * 128], in_=nrb[:, k * 128:(k + 1) * 128], identity=self.identb[:]),
                            reads=[bnrb, self.BID], writes=[bpbt])
                    so = stg[:, 0:ncol]
                    S.op("act", E("activation", out=so, in_=pbt[:, 0:ncol], func=AF.Copy),
                         reads=[bpbt], writes=[bstg])
                    if name == "QA":
                        S.dma("sp", self.QA_d[:, i, :], so, reads=[bstg], writes=[self.B_QAd])
                    elif name == "QB":
                        S.dma("sp", self.QB_d[:, i, :], so, reads=[bstg], writes=[self.B_QBd])
                    else:
                        s0 = 2 if name == "KB" else 0
                        S.dma("sp", self.kt_own_view()[:, s0:s0 + nblk, i * 128:(i + 1) * 128],
                              so.rearrange("p (s t) -> p s t", s=nblk), reads=[bstg], writes=[self.B_KTown])
                elif name == "RAW1":
                    S.op("act", E("activation", out=nrb[:, 0:256], in_=pf[:, 0:256], func=AF.Copy),
                         reads=[bpf], writes=[bnrb])
                    pbt, bpbt = self.next_pbt()
                    for k in range(2):
                        S.op("pe", E("transpose",
                            out=pbt[:, k * 128:(k + 1) * 128], in_=nrb[:, k * 128:(k + 1) * 128], identity=self.identb[:]),
                            reads=[bnrb, self.BID], writes=[bpbt])
                    sv = stgv[:, 0:260].rearrange("p (h c) -> p h c", c=65)
                    S.op("dve", E("tensor_copy", out=sv[:, :, 0:64],
                                                                      in_=pf[:, 256:512].rearrange("p (h d) -> p h d", d=64)),
                         reads=[bpf], writes=[bstgv])
                    sk = stg[:, 0:256]
                    S.op("act", E("activation", out=sk, in_=pbt[:, 0:256], func=AF.Copy),
                         reads=[bpbt], writes=[bstg])
                    S.dma("sp", self.kt_own_view()[:, 6:8, i * 128:(i + 1) * 128], sk.rearrange("p (s t) -> p s t", s=2),
                          reads=[bstg], writes=[self.B_KTown])
                    S.dma("sp", self.V_own[i * 128:(i + 1) * 128, 0:260], stgv[:, 0:260], reads=[bstgv], writes=[self.B_Vown])
                elif name == "VB":
                    sv = stgv[:, 0:520].rearrange("p (h c) -> p h c", c=65)
                    S.op("dve", E("tensor_copy", out=sv[:, :, 0:64],
                                                                      in_=pf[:, 0:512].rearrange("p (h d) -> p h d", d=64)),
                         reads=[bpf], writes=[bstgv])
                    S.dma("sp", self.V_own[i * 128:(i + 1) * 128, 260:780], stgv[:, 0:520], reads=[bstgv], writes=[self.B_Vown])
                elif name == "GA":
                    S.op("act", E("activation", out=self.GA[:, i, :], in_=pf[:, 0:24], func=AF.Sigmoid),
                         reads=[bpf], writes=[self.BGA])
        if self.mode == "A":
            S.dma("sp", self.GA_d[:, :, :], self.GA[:], reads=[self.BGA], writes=[self.B_GAd])
        AR.release(mk)

    def kt_own_view(self):
        return self.KT_own

    def kt_all_view(self):
        return self.KT_all

    def v_all_view(self):
        return self.V_all

    def exchange(self):
        S = self.S
        rg = [[0, 1, 2, 3], [4, 5, 6, 7]]
        for (src, dst, bs, bd) in [(self.KT_own, self.KT_allf, self.B_KTown, self.B_KTall),
                                   (self.V_own, self.V_allf, self.B_Vown, self.B_Vall)]:
            S._deps("pool", [bs], [bd])
            k = S.dma_next
            S.dma_next = (k + 1) % S.n_dma
            if S.dma_val[k] > 0:
                S._wait("pool", ("d", k, S.dma_val[k]))
            S.dma_val[k] += 16
            val = S.dma_val[k]
            sem = S.dma_sems[k]
            S.prog["pool"].append(lambda src=src, dst=dst, sem=sem: self.nc.gpsimd.collective_compute(
                "AllGather", ALU.bypass, replica_groups=rg, ins=[src[:, :]], outs=[dst[:, :]]).then_inc(sem, 16))
            tok = ("d", k, val)
            bd.lw = tok
            bd.rd = {}
            bs.rd[("dq", k)] = tok

    def load_kT(self, dst, bdst, slot, ntile=64):
        S = self.S
        kv = self.kt_all_view()
        ni = ntile // 4
        d4 = dst[:, 0:ntile * 128].rearrange("p (i r t) -> p i r t", r=4, t=128)
        for r in range(4):
            src = kv[r, :, slot, 0:ni * 128].rearrange("p (i t) -> p i t", t=128)
            S.dma("sp", d4[:, :, r, :], src, reads=[self.B_KT4[r]], writes=[bdst])

    def load_v(self, dst4, bdst, c0, ncol, i0, ni):
        S = self.S
        vv = self.v_all_view()
        for r in range(4):
            src = vv[r, i0 * 128:(i0 + ni) * 128, c0:c0 + ncol].rearrange("(i t) c -> t i c", t=128)
            S.dma("sp", dst4[:, :, r, :], src, reads=[self.B_V4[r]], writes=[bdst])

    def phase_b0(self):
        S = self.S
        AR = self.AR
        mk = AR.mark()
        self.load_gains()
        KCT, bKCT = AR.carve("KCT", 8192)
        VCT, bVCT = AR.carve("VCT", 8192)
        self.load_kT(KCT, bKCT, 6)
        self.load_kT(VCT, bVCT, 7)
        W1, bW1 = AR.carve("W1", 32 * 128)
        W13 = W1.rearrange("p (q h) -> p q h", q=32)
        W2, bW2 = AR.carve("W2", 64)
        PET, bPET = AR.carve("PET", 32)
        HT, bHT = AR.carve("HTc", 512)
        KCN, bKCN = AR.carve("KCN", 4 * 128)
        KCN4 = KCN.rearrange("p (n g d) -> p n g d", n=4, g=2)
        Bw = Buf("cmpw_d")
        S.op("pool", E("memset", HT[:, 508:512], 0.0), writes=[bHT])
        S.dma("pool", PET[0:64, :], self.w("nsa_cmp_pe").rearrange("q d -> d q"), reads=[Bw], writes=[bPET],
              allow_slow_non_contiguous=True)
        for kv, (w1n, w2n, SRC, bSRC) in enumerate([("nsa_cmp_k_w1", "nsa_cmp_k_w2", KCT, bKCT),
                                                    ("nsa_cmp_v_w1", "nsa_cmp_v_w2", VCT, bVCT)]):
            w1 = self.w(w1n).rearrange("(q d) h -> d q h", d=64)
            S.dma("pool", W13[0:64, :, :], w1, reads=[Bw], writes=[bW1])
            S.dma("pool", W13[64:128, :, :], w1, reads=[Bw], writes=[bW1])
            S.dma("pool", W2[:, :], self.w(w2n)[:, :], reads=[Bw], writes=[bW2])
            pfb, bpfb = self.PF[4], self.BPF[4]
            for q in range(32):
                S.op("pe", E("matmul", pfb[:, 0:1], lhsT=W13[0:64, q, :], rhs=PET[0:64, q:q + 1],
                                                   start=(q == 0), stop=(q == 31)), reads=[bW1, bPET], writes=[bpfb])
            bcol, bbcol = self.sm(1)
            S.op("act", E("activation", out=bcol, in_=pfb[:, 0:1], func=AF.Copy), reads=[bpfb], writes=[bbcol])
            for g in range(2):
                pf, bpf = self.PF[g], self.BPF[g]
                for q in range(32):
                    S.op("pe", E("matmul",
                        pf[:, 0:511], lhsT=W13[64 * g:64 * g + 64, q, :], rhs=SRC[64 * g:64 * g + 64, q:q + 16 * 510 + 1:16],
                        start=(q == 0), stop=(q == 31)), reads=[bW1, bSRC], writes=[bpf])
                Z, Z2, U = self.NRS[0][:, 0:511], self.NRS[1][:, 0:511], self.NRS[2][:, 0:511]
                B0, B1, B2 = self.BNRS[0:3]
                S.op("act", E("activation", out=Z, in_=pf[:, 0:511], func=AF.Identity, bias=bcol),
                     reads=[bpf, bbcol], writes=[B0])
                S.op("dve", E("tensor_tensor", out=Z2, in0=Z, in1=Z, op=ALU.mult), reads=[B0], writes=[B1])
                S.op("dve", E("tensor_scalar", out=Z2, in0=Z2, scalar1=0.044715, scalar2=1.0, op0=ALU.mult, op1=ALU.add),
                     reads=[B1], writes=[B1])
                S.op("dve", E("tensor_tensor", out=U, in0=Z2, in1=Z, op=ALU.mult), reads=[B0, B1], writes=[B2])
                S.op("act", E("activation", out=U, in_=U, func=AF.Sigmoid, scale=1.5957691216057308),
                     reads=[B2], writes=[B2])
                S.op("dve", E("tensor_tensor", out=HT[:, 0:511], in0=Z, in1=U, op=ALU.mult),
                     reads=[B0, B2], writes=[bHT])
                pf2, bpf2 = self.PF[2 + g], self.BPF[2 + g]
                for nt in range(4):
                    S.op("pe", E("matmul", pf2[:, nt * 64:(nt + 1) * 64], lhsT=HT[:, nt * 128:(nt + 1) * 128],
                                                                  rhs=W2[:, :], start=True, stop=True),
                         reads=[bHT, bW2], writes=[bpf2])
                if kv == 0:
                    self.nr_post(pf2[:, 0:256].rearrange("p (n d) -> p n d", n=4), bpf2, 4, 2,
                                 self.COSC[:, :, :], self.SINC[:, :, :], KCN4[:, :, g, :], bKCN)
                else:
                    S.op("act", E("activation", out=self.OVV[:, :, g, 128:192],
                                                                     in_=pf2[:, 0:256].rearrange("p (n d) -> p n d", n=4),
                                                                     func=AF.Copy), reads=[bpf2], writes=[self.BOVV])
            if kv == 0:
                pbt, bpbt = self.next_pbt()
                for nt in range(4):
                    S.op("pe", E("transpose", out=pbt[:, nt * 128:(nt + 1) * 128],
                                                                    in_=KCN[:, nt * 128:(nt + 1) * 128], identity=self.identb[:]),
                         reads=[bKCN, self.BID], writes=[bpbt])
                S.op("act", E("activation", out=self.KcT[:, :], in_=pbt[:, 0:512], func=AF.Copy),
                     reads=[bpbt], writes=[self.BKcT])
        AR.release(mk)

    def pv_flush(self):
        for spec, R, Wr in self._pv_pend:
            self.S.op("pe", spec, reads=R, writes=Wr)
        self._pv_pend = []

    def qk_exp(self, sbank, bsbank, mms, pt, bpt, ncols=512):
        S = self.S
        for fn, R in mms:
            S.op("pe", fn, reads=R, writes=[bsbank])
        S.op("act", E("activation", out=pt[:, 0:ncols], in_=sbank[:, 0:ncols], func=AF.Exp), reads=[bsbank], writes=[bpt])

    def phase_nsa(self):
        S = self.S
        AR = self.AR
        mk = AR.mark()
        YS = [AR.carve(f"YS{k}", 256) for k in range(2)]
        ysk = 0
        KsT, bKsT = AR.carve("KsT", 8192)
        VsA, bVsA = AR.carve("VsA", 64 * 130)
        VsA4 = VsA.rearrange("p (i r c) -> p i r c", r=4, c=130)
        self.load_kT(KsT, bKsT, 0)
        self.load_v(VsA4, bVsA, 0, 130, 0, 16)
        QA = [AR.carve(f"QAt{k}", 1024) for k in range(2)]
        for k in range(2):
            S.op("pool", E("memset", QA[k][0], 0.0), writes=[QA[k][1]])
        PT = [AR.carve(f"PT{k}", 512) for k in range(2)]
        KW = [AR.carve(f"KW{k}", 8 * 128) for k in range(2)]
        VW = [AR.carve(f"VW{k}", 8 * 130) for k in range(2)]
        BT, bBT = AR.carve("BT", 128)
        YAb, bYAb = AR.carve("YAb", 256)
        IMP, BIAS = self.NRS[0][:, 0:128], self.NRS[1][:, 0:128]
        IMP2 = self.NRS[1][:, 128:256]
        YAf = self.NRS[2][:, 0:256].rearrange("p (h d) -> p h d", h=4)
        bIMP, bBIAS, bYAf = self.BNRS[0], self.BNRS[1], self.BNRS[2]
        ptk = 0
        sk = 0
        for i in range(NT):
            qaz, bqa = QA[i % 2]
            qaz3 = qaz.rearrange("p (g n) -> p g n", g=2)
            for g in range(2):
                S.dma("sp", qaz3[64 * g:64 * g + 64, g, :], self.QA_d[64 * g:64 * g + 64, i, :], reads=[self.B_QAd], writes=[bqa])
            kw, bkw = KW[i % 2]
            vw, bvw = VW[i % 2]
            i0 = max(i - 1, 0)
            ni = i + 1 - i0
            kv = self.kt_all_view()
            kw4 = kw[:, 0:ni * 512].rearrange("p (i r t) -> p i r t", r=4, t=128)
            for r in range(4):
                S.dma("sp", kw4[:, :, r, :], kv[r, :, 1, i0 * 128:(i0 + ni) * 128].rearrange("p (i t) -> p i t", t=128),
                      reads=[self.B_KT4[r]], writes=[bkw])
            vw4 = vw[:, 0:ni * 520].rearrange("p (i r c) -> p i r c", r=4, c=130)
            self.load_v(vw4, bvw, 130, 130, i0, ni)
            for g in range(2):
                qa = qaz3[:, g, :]
                ntn = i // 4 + 1
                oc = [(self.PF[2], self.BPF[2]), (self.PF[5], self.BPF[5])]
                for nt in range(ntn):
                    sb_, bsb = self.PF[sk % 2], self.BPF[sk % 2]
                    sk += 1
                    pt, bpt = PT[ptk % 2]
                    ptk += 1
                    mms = []
                    msk = None
                    if nt == i // 4:
                        msk = self.CMA[:, i % 4, :]
                    elif nt == i // 4 - 1 and i % 4 == 0:
                        msk = self.CMB[:, :]
                    mms.append((E("matmul",
                        sb_[:, :], lhsT=self.KcT[:, nt * 128:(nt + 1) * 128], rhs=qa, start=True, stop=(msk is None)),
                        [self.BKcT, bqa]))
                    if msk is not None:
                        mms.append((E("matmul",
                            sb_[:, :], lhsT=self.identb[:], rhs=bc(msk.unsqueeze(1), [128, 4, 128]), start=False, stop=True),
                            [self.BID, self.BCST]))
                    self.qk_exp(sb_, bsb, mms, pt, bpt)
                    self.pv_flush()
                    for h in range(4):
                        ob, bob = oc[h // 2]
                        self._pv_pend.append((E("matmul",
                            ob[:, (h % 2) * 193:(h % 2) * 193 + 193], lhsT=pt[:, h * 128:(h + 1) * 128],
                            rhs=self.OVV[:, nt, g, :], start=(nt == 0 and h % 2 == 0), stop=(nt == ntn - 1),
                            skip_group_check=True), [bpt, self.BOVV], [bob]))
                self.pv_flush()
                rsc, brsc = self.sm(4)
                scl, bscl = self.sm(4)
                for hh2 in range(2):
                    ob, bob = oc[hh2]
                    S.op("dve", E("tensor_scalar", out=rsc[:, 2 * hh2:2 * hh2 + 2], in0=ob[:, 192:386:193], scalar1=1.0e-30,
                                  scalar2=None, op0=ALU.add), reads=[bob], writes=[brsc])
                S.op("dve", E("reciprocal", out=rsc, in_=rsc), reads=[brsc], writes=[brsc])
                S.op("dve", E("tensor_scalar", out=IMP, in0=oc[0][0][:, 0:128], scalar1=rsc[:, 0:1], scalar2=None,
                                                      op0=ALU.mult), reads=[oc[0][1], brsc], writes=[bIMP])
                for h in range(1, 4):
                    ob, bob = oc[h // 2]
                    S.op("dve", E("scalar_tensor_tensor",
                        out=IMP, in0=ob[:, (h % 2) * 193:(h % 2) * 193 + 128], scalar=rsc[:, h:h + 1], in1=IMP,
                        op0=ALU.mult, op1=ALU.add), reads=[bob, brsc, bIMP], writes=[bIMP])
                gsl = lambda b: self.GA[:, i, 12 * g + b:12 * g + 12:3]
                S.op("dve", E("tensor_tensor", out=scl, in0=rsc, in1=gsl(0), op=ALU.mult),
                     reads=[brsc, self.BGA], writes=[bscl])
                for h in range(4):
                    ob, bob = oc[h // 2]
                    S.op("act", E("activation",
                        out=YAf[:, h, :], in_=ob[:, (h % 2) * 193 + 128:(h % 2) * 193 + 192], func=AF.Copy, scale=scl[:, h:h + 1]),
                        reads=[bob, bscl], writes=[bYAf])
                wsl = slice(128 - 8 * i, 256 - 8 * i)
                S.op("dve", E("tensor_tensor", out=IMP, in0=IMP, in1=self.FWJ[:, wsl], op=ALU.max),
                     reads=[bIMP, self.BCST], writes=[bIMP])
                S.op("dve", E("tensor_tensor", out=IMP, in0=IMP, in1=self.CWJ[:, wsl], op=ALU.min),
                     reads=[bIMP, self.BCST], writes=[bIMP])
                S.op("dve", E("memset", IMP[:, 0:1], 1.0e4), reads=[], writes=[bIMP])
                m8, bm8 = self.sm(16)
                S.op("dve", E("max", out=m8[:, 0:8], in_=IMP), reads=[bIMP], writes=[bm8])
                S.op("dve", E("match_replace", out=IMP2, in_to_replace=m8[:, 0:8], in_values=IMP, imm_value=-2.0e30),
                     reads=[bIMP, bm8], writes=[bBIAS])
                S.op("dve", E("max", out=m8[:, 8:16], in_=IMP2), reads=[bBIAS], writes=[bm8])
                S.op("dve", E("tensor_scalar", out=BIAS, in0=IMP, scalar1=m8[:, 15:16], scalar2=MASKV,
                                                      op0=ALU.is_lt, op1=ALU.mult), reads=[bIMP, bm8], writes=[bBIAS])
                pf4, bpf4 = self.PF[4], self.BPF[4]
                S.op("pe", E("transpose", out=pf4[:, 0:128], in_=BIAS, identity=self.identf[:]),
                     reads=[bBIAS, self.BID], writes=[bpf4])
                S.op("act", E("activation", out=BT, in_=pf4[:, 0:128], func=AF.Copy), reads=[bpf4], writes=[bBT])
                for br in (1, 2):
                    ob, bob = (self.PF[3], self.BPF[3])
                    if br == 1:
                        kts = list(range(0, 4 * i + 4))
                    else:
                        kts = list(range(max(4 * i - 4, 0), 4 * i + 4))
                    for idx, kt in enumerate(kts):
                        sb_, bsb = self.PF[sk % 2], self.BPF[sk % 2]
                        sk += 1
                        pt, bpt = PT[ptk % 2]
                        ptk += 1
                        mms = []
                        if br == 1:
                            mms.append((E("matmul",
                                sb_[:, :], lhsT=KsT[:, kt * 128:(kt + 1) * 128], rhs=qa, start=True, stop=False),
                                [bKsT, bqa]))
                            for half in range(2):
                                mms.append((E("matmul",
                                    sb_[64 * half:64 * half + 64, :], lhsT=bc(self.identb[:, 2 * kt + half:2 * kt + half + 1], [128, 64]),
                                    rhs=bc(BT.unsqueeze(1), [128, 4, 128]), start=False, stop=(half == 1 and kt < 4 * i),
                                    skip_group_check=True), [self.BID, bBT]))
                            if kt >= 4 * i:
                                mms.append((E("matmul",
                                    sb_[:, :], lhsT=self.identb[:], rhs=bc(self.DIAG[:, kt - 4 * i, :].unsqueeze(1), [128, 4, 128]),
                                    start=False, stop=True), [self.BID, self.BCST]))
                            vrhs = VsA4[:, kt // 4, kt % 4, g * 65:g * 65 + 65]
                            bv = bVsA
                        else:
                            slot = kt - 4 * i0
                            mms.append((E("matmul",
                                sb_[:, :], lhsT=kw[:, slot * 128:(slot + 1) * 128], rhs=qa, start=True, stop=False),
                                [bkw, bqa]))
                            mms.append((E("matmul",
                                sb_[:, :], lhsT=self.identb[:], rhs=bc(self.WIN[:, kt - (4 * i - 4), :].unsqueeze(1), [128, 4, 128]),
                                start=False, stop=True), [self.BID, self.BCST]))
                            vrhs = vw4[:, slot // 4, slot % 4, g * 65:g * 65 + 65]
                            bv = bvw
                        self.qk_exp(sb_, bsb, mms, pt, bpt)
                        self.pv_flush()
                        nk = len(kts)
                        for h in range(4):
                            self._pv_pend.append((E("matmul",
                                ob[:, h * 65:h * 65 + 65], lhsT=pt[:, h * 128:(h + 1) * 128], rhs=vrhs,
                                start=(idx == 0 and h == 0), stop=(idx == nk - 1), skip_group_check=True),
                                [bpt, bv], [bob]))
                    self.pv_flush()
                    rs2, brs2 = self.sm(4)
                    sc2, bsc2 = self.sm(4)
                    S.op("dve", E("reciprocal", out=rs2, in_=ob[:, 64:260:65]), reads=[bob], writes=[brs2])
                    S.op("dve", E("tensor_tensor", out=sc2, in0=rs2, in1=gsl(br), op=ALU.mult),
                         reads=[brs2, self.BGA], writes=[bsc2])
                    for h in range(4):
                        S.op("dve", E("scalar_tensor_tensor",
                            out=YAf[:, h, :], in0=ob[:, h * 65:h * 65 + 64], scalar=sc2[:, h:h + 1], in1=YAf[:, h, :],
                            op0=ALU.mult, op1=ALU.add), reads=[bob, bsc2, bYAf], writes=[bYAf])
                S.op("act", E("activation", out=YAb, in_=self.NRS[2][:, 0:256], func=AF.Copy), reads=[bYAf], writes=[bYAb])
                pbt, bpbt = self.next_pbt()
                for k in range(2):
                    S.op("pe", E("transpose", out=pbt[:, k * 128:(k + 1) * 128],
                                                                  in_=YAb[:, k * 128:(k + 1) * 128], identity=self.identb[:]),
                         reads=[bYAb, self.BID], writes=[bpbt])
                ys, bys = YS[ysk % 2]
                ysk += 1
                S.op("dve", E("tensor_copy", out=ys, in_=pbt[:, 0:256]), reads=[bpbt], writes=[bys])
                S.dma("sp", self.YT_d[:, 2 * g:2 * g + 2, i * 128:(i + 1) * 128], ys.rearrange("p (c t) -> p c t", c=2),
                      reads=[bys], writes=[self.B_YTd[i]])
        AR.release(mk)

    def phase_moba(self):
        S = self.S
        AR = self.AR
        mk = AR.mark()
        KB_, bKB = AR.carve("KbT", 8192)
        VB_, bVB = AR.carve("VbA", 64 * 130)
        VB4 = VB_.rearrange("p (i r c) -> p i r c", r=4, c=130)
        QB = [AR.carve(f"QBt{k}", 256) for k in range(2)]
        for k in range(2):
            S.op("pool", E("memset", QB[k][0], 0.0), writes=[QB[k][1]])
        PT = [AR.carve(f"PTm{k}", 512) for k in range(2)]
        KMb, bKMb = AR.carve("KMb", 32)
        BTm, bBTm = AR.carve("BTm", 128)
        YBb, bYBb = AR.carve("YBb", 128)
        YS = [AR.carve(f"YSm{k}", 128) for k in range(2)]
        ysk = 0
        KM = self.NRS[0][:, 0:32]
        bKM = self.BNRS[0]
        GM = self.NRS[1][:, 0:64].rearrange("p (h n) -> p h n", h=2)
        bGM = self.BNRS[1]
        BIt, bBI = AR.carve("BIb", 128)
        BI = BIt[:, 0:64].rearrange("p (h n) -> p h n", h=2)
        S.op("pool", E("memset", BIt, 0.0), writes=[bBI])
        ptk = 0
        sk = 0
        qk = 0
        import os
        km_stage = int(os.environ.get("KM_STAGE", "9"))
        km_tiles = int(os.environ.get("KM_TILES", "16"))
        km_pairs = int(os.environ.get("KM_PAIRS", "4"))
        for p in range(km_pairs):
            self.load_kT(KB_, bKB, 2 + p)
            self.load_v(VB4, bVB, 260 + 130 * p, 130, 0, 16)
            S.op("dve", E("tensor_reduce", out=KM, in_=KB_.rearrange("p (n k) -> p n k", k=256), axis=AX.X, op=ALU.add),
                 reads=[bKB], writes=[bKM])
            S.op("act", E("activation", out=KMb, in_=KM, func=AF.Copy, scale=1.0 / 256), reads=[bKM], writes=[bKMb])
            for i in range(km_tiles if km_stage > 0 else 0):
                qbz, bqb = QB[qk % 2]
                qbz3 = qbz.rearrange("p (h n) -> p h n", h=2)
                qk += 1
                for hh in range(2):
                    S.dma("sp", qbz3[64 * hh:64 * hh + 64, hh, :], self.QB_d[64 * hh:64 * hh + 64, i, p * 128:(p + 1) * 128],
                          reads=[self.B_QBd], writes=[bqb])
                pf4, bpf4 = self.PF[4], self.BPF[4]
                for hh in range(2):
                    hs = slice(64 * hh, 64 * hh + 64)
                    S.op("pe", E("matmul", pf4[:, hh * 32:hh * 32 + 32], lhsT=qbz3[:, hh, :], rhs=KMb[:, :],
                                                                       start=True, stop=True, skip_group_check=True),
                         reads=[bqb, bKMb], writes=[bpf4])
                wsl = slice(32 - 2 * i, 64 - 2 * i)
                S.op("dve", E("tensor_tensor", out=GM, in0=pf4[:, 0:64].rearrange("p (h n) -> p h n", h=2),
                                                               in1=bc(self.CWM[:, wsl].unsqueeze(1), [128, 2, 32]), op=ALU.min),
                     reads=[bpf4, self.BCST], writes=[bGM])
                m8, bm8 = self.sm(16)
                for hh in range(2):
                    S.op("dve", E("max", out=m8[:, 8 * hh:8 * hh + 8], in_=GM[:, hh, :]),
                         reads=[bGM], writes=[bm8])
                for hh in range(2):
                    S.op("dve", E("tensor_scalar", out=BI[:, hh, :], in0=GM[:, hh, :],
                                                                        scalar1=m8[:, 8 * hh + 2:8 * hh + 3], scalar2=MASKV,
                                                                        op0=ALU.is_lt, op1=ALU.mult),
                         reads=[bGM, bm8], writes=[bBI])
                S.op("dve", E("tensor_tensor", out=BI, in0=BI, in1=bc(self.NOJ[:, wsl].unsqueeze(1), [128, 2, 32]),
                                                               op=ALU.mult), reads=[bBI, self.BCST], writes=[bBI])
                pbt, bpbt = self.next_pbt()
                S.op("pe", E("transpose", out=pbt[:, 0:128], in_=BIt, identity=self.identb[:]),
                     reads=[bBI, self.BID], writes=[bpbt])
                S.op("act", E("activation", out=BTm, in_=pbt[:, 0:128], func=AF.Copy), reads=[bpbt], writes=[bBTm])
                if km_stage < 2:
                    continue
                ob, bob = self.PF[2 + i % 2], self.BPF[2 + i % 2]
                nblk = 2 * i + 2
                for blk in range(nblk):
                    sb_, bsb = self.PF[sk % 2], self.BPF[sk % 2]
                    sk += 1
                    pt, bpt = PT[ptk % 2]
                    ptk += 1
                    mms = []
                    for kt2 in range(2):
                        for hh in range(2):
                            hs = slice(64 * hh, 64 * hh + 64)
                            kt = 2 * blk + kt2
                            col = (kt2 * 2 + hh) * 128
                            first = (kt2 == 0 and hh == 0)
                            mms.append((E("matmul",
                                sb_[:, col:col + 128], lhsT=KB_[:, kt * 128:(kt + 1) * 128], rhs=qbz3[:, hh, :],
                                start=first, stop=False, skip_group_check=True), [bKB, bqb]))
                    lb = blk < 2 * i
                    if km_stage >= 3:
                        sb4 = sb_[:, :].rearrange("p (a h q) -> p a h q", a=2, h=2)
                        for hh in range(2):
                            mms.append((E("matmul", sb4[:, :, hh, :],
                                          lhsT=bc(self.identb[:, 32 * hh + blk:32 * hh + blk + 1], [128, 128]),
                                          rhs=bc(BTm.unsqueeze(1), [128, 2, 128]), start=False, stop=(lb and hh == 1),
                                          skip_group_check=True), [self.BID, bBTm]))
                    if blk >= 2 * i and km_stage >= 4:
                        for kt2 in range(2):
                            jj = 2 * (blk - 2 * i) + kt2
                            mms.append((E("matmul",
                                sb_[:, kt2 * 256:kt2 * 256 + 256], lhsT=self.identb[:],
                                rhs=bc(self.DIAG[:, jj, :].unsqueeze(1), [128, 2, 128]), start=False, stop=(kt2 == 1),
                                skip_group_check=True), [self.BID, self.BCST]))
                    self.qk_exp(sb_, bsb, mms, pt, bpt)
                    self.pv_flush()
                    for kt2 in range(2):
                        for hh in range(2):
                            kt = 2 * blk + kt2
                            col = (kt2 * 2 + hh) * 128
                            first = (blk == 0 and kt2 == 0 and hh == 0)
                            last = (blk == nblk - 1 and kt2 == 1)
                            self._pv_pend.append((E("matmul",
                                ob[:, hh * 65:hh * 65 + 65], lhsT=pt[:, col:col + 128], rhs=VB4[:, kt // 4, kt % 4, hh * 65:hh * 65 + 65],
                                start=first, stop=last, skip_group_check=True), [bpt, bVB], [bob]))
                self.pv_flush()
                rs2, brs2 = self.sm(2)
                S.op("dve", E("reciprocal", out=rs2, in_=ob[:, 64:130:65]), reads=[bob], writes=[brs2])
                for hh in range(2):
                    S.op("act", E("activation", out=YBb[:, hh * 64:hh * 64 + 64],
                                                                              in_=ob[:, hh * 65:hh * 65 + 64], func=AF.Copy,
                                                                              scale=rs2[:, hh:hh + 1]),
                         reads=[bob, brs2], writes=[bYBb])
                pbt, bpbt = self.next_pbt()
                S.op("pe", E("transpose", out=pbt[:, 0:128], in_=YBb[:, :], identity=self.identb[:]),
                     reads=[bYBb, self.BID], writes=[bpbt])
                ys, bys = YS[ysk % 2]
                ysk += 1
                S.op("dve", E("tensor_copy", out=ys, in_=pbt[:, 0:128]), reads=[bpbt], writes=[bys])
                S.dma("sp", self.YT_d[:, 4 + p, i * 128:(i + 1) * 128], ys, reads=[bys], writes=[self.B_YTd[i]])
        AR.release(mk)

    def phase_b2(self):
        S = self.S
        AR = self.AR
        mk = AR.mark()
        Bw = Buf("b2w_d")
        WG, bWG = AR.carve("WG", 8 * 2048)
        WG3 = WG.rearrange("p (k n) -> p k n", k=8)
        WUA, bWUA = AR.carve("WUA", 4 * 1024)
        WUB, bWUB = AR.carve("WUB", 4 * 1024)
        WO, bWO = AR.carve("WO", 8 * 1024)
        WUA3 = WUA.rearrange("p (k n) -> p k n", k=4)
        WUB3 = WUB.rearrange("p (k n) -> p k n", k=4)
        WO3 = WO.rearrange("p (k n) -> p k n", k=8)
        w_in = self.w("w_in")
        for h2 in range(4):
            S.dma("pool", WG3[:, :, h2 * 512:(h2 + 1) * 512],
                  w_in[:, 2840 + h2 * 512:2840 + (h2 + 1) * 512].rearrange("(k p) n -> p k n", p=128), reads=[Bw], writes=[bWG])
        S.dma("pool", WUA3, self.w("w_up_nsa").rearrange("(k p) n -> p k n", p=128), reads=[Bw], writes=[bWUA])
        S.dma("pool", WUB3, self.w("w_up_moba").rearrange("(k p) n -> p k n", p=128), reads=[Bw], writes=[bWUB])
        for h2 in range(2):
            S.dma("pool", WO3[:, :, h2 * 512:(h2 + 1) * 512],
                  self.w("w_out")[:, h2 * 512:(h2 + 1) * 512].rearrange("(k p) n -> p k n", p=128), reads=[Bw], writes=[bWO])
        YTt = [AR.carve(f"YTt{k}", 1024) for k in range(2)]
        hTt = [AR.carve(f"hTt{k}", 1024) for k in range(2)]
        MT = [AR.carve(f"MT{k}", 1024) for k in range(2)]
        junk, bjunk = AR.carve("junk", 1024)
        hb, bhb = AR.carve("hb", 1024)
        SG = self.NRS[0][:, 0:256].rearrange("p (a t) -> p a t", a=2)
        PR = self.NRS[1][:, 0:256].rearrange("p (a t) -> p a t", a=2)
        bSG, bPR = self.BNRS[0], self.BNRS[1]
        bk = 0
        for i in range(NT):
            ytt, bytt = YTt[i % 2]
            ytt3 = ytt.rearrange("p (c t) -> p c t", c=8)
            S.dma("sp", ytt3, self.YT_d[:, :, i * 128:(i + 1) * 128], reads=[self.B_YTd[i]], writes=[bytt])
            ht, bht = hTt[i % 2]
            ht3 = ht.rearrange("p (k t) -> p k t", k=8)
            self.norm_T(i, ht3, bht, junk, bjunk, hb, bhb)
            mt, bmt = MT[i % 2]
            mt3 = mt.rearrange("p (k t) -> p k t", k=8)
            ts_ = slice(i * 128, (i + 1) * 128)
            for mc in range(8):
                pf, bpf = self.PF[bk % 2], self.BPF[bk % 2]
                bk += 1
                ms = slice(mc * 128, (mc + 1) * 128)
                first = True
                for slot, (Wt, nk, rhs_fn, Rw) in enumerate([
                        (WG3, 8, lambda kc: ht3[:, kc, :], [bWG, bht]),
                        (WUA3, 4, lambda kc: ytt3[:, kc, :], [bWUA, bytt]),
                        (WG3, 8, lambda kc: ht3[:, kc, :], [bWG, bht]),
                        (WUB3, 4, lambda kc: ytt3[:, 4 + kc, :], [bWUB, bytt])]):
                    off = 1024 if slot == 2 else 0
                    for kc in range(nk):
                        last = (slot == 3 and kc == nk - 1)
                        S.op("pe", E("matmul",
                            pf[:, slot * 128:(slot + 1) * 128], lhsT=Wt[:, kc, off + mc * 128:off + (mc + 1) * 128], rhs=rhs_fn(kc),
                            start=first, stop=last, skip_group_check=True), reads=Rw, writes=[bpf])
                        first = False
                pf3 = pf[:, :].rearrange("p (a t) -> p a t", a=4)
                S.op("act", E("activation", out=SG, in_=pf3[:, 0:4:2, :], func=AF.Sigmoid), reads=[bpf], writes=[bSG])
                S.op("dve", E("tensor_tensor", out=PR, in0=SG, in1=pf3[:, 1:4:2, :], op=ALU.mult),
                     reads=[bSG, bpf], writes=[bPR])
                S.op("pool", E("tensor_tensor", out=mt3[:, mc, :], in0=PR[:, 0, :], in1=PR[:, 1, :], op=ALU.add),
                     reads=[bPR], writes=[bmt])
            for h2 in range(2):
                po, bpo = self.PF[2 + h2], self.BPF[2 + h2]
                for mc in range(8):
                    S.op("pe", E("matmul",
                        po[:, :], lhsT=mt3[:, mc, :], rhs=WO3[:, mc, h2 * 512:(h2 + 1) * 512], start=(mc == 0), stop=(mc == 7)),
                        reads=[bmt, bWO], writes=[bpo])
                hs = slice(h2 * 512, (h2 + 1) * 512)
                tmp, btmp = self.NRS[2 + h2], self.BNRS[2 + h2]
                S.op("dve", E("tensor_tensor", out=tmp[:, :], in0=po[:, :], in1=self.ADA[:, 2, hs], op=ALU.mult),
                     reads=[bpo, self.BADA], writes=[btmp])
                S.op("pool", E("tensor_tensor", out=self.X[:, i, hs], in0=self.X[:, i, hs], in1=tmp[:, :], op=ALU.add),
                     reads=[btmp, self.BX[i]], writes=[self.BX[i]])
        AR.release(mk)

    def phase_ffn(self):
        S = self.S
        AR = self.AR
        mk = AR.mark()
        Bw = Buf("ffnw_d")
        hT, _ = AR.carve("hT_all", 8 * 2048)
        hT3 = hT.rearrange("p (k t) -> p k t", k=8)
        BhT = [Buf(f"hTf{i}", AR.pend) for i in range(NT)]
        junk, bjunk = AR.carve("junk", 1024)
        hb, bhb = AR.carve("hb", 1024)
        for i in range(NT):
            self.norm_T(i, hT3[:, :, i * 128:(i + 1) * 128], BhT[i], junk, bjunk, hb, bhb)
        G = 3
        WGU = [AR.carve(f"WGU{k}", 8 * 2 * G * 128) for k in range(2)]
        WOF = [AR.carve(f"WOF{k}", G * 1024) for k in range(2)]
        ACTT, bACT_ = AR.carve("ACTT", G * 2048)
        ACT3 = ACTT.rearrange("p (m t) -> p m t", m=G)
        BACT = [Buf(f"act{t}", AR.pend) for t in range(4)]
        SI = [AR.carve(f"SI{k}", 512) for k in range(2)]
        w1 = self.w("w_ffn_in")
        w2 = self.w("w_ffn_out")
        groups = []
        m0 = 0
        while m0 < 22:
            groups.append((m0, min(G, 22 - m0)))
            m0 += G
        bk = 0
        for gi, (m0, gm) in enumerate(groups):
            wgu, bwgu = WGU[gi % 2]
            wgu4 = wgu.rearrange("p (k a n) -> p k a n", k=8, a=2)
            wof, bwof = WOF[gi % 2]
            wof3 = wof.rearrange("p (m n) -> p m n", m=G)
            for a in range(2):
                S.dma("pool", wgu4[:, :, a, 0:gm * 128],
                      w1[:, a * DFF + m0 * 128:a * DFF + (m0 + gm) * 128].rearrange("(k p) n -> p k n", p=128),
                      reads=[Bw], writes=[bwgu])
            S.dma("pool", wof3[:, 0:gm, :], w2[m0 * 128:(m0 + gm) * 128, :].rearrange("(m p) n -> p m n", p=128),
                  reads=[Bw], writes=[bwof])
            for tg in range(4):
                tsl = slice(tg * 512, (tg + 1) * 512)
                for m in range(gm):
                    pg, bpg = self.PF[(bk % 2) * 2], self.BPF[(bk % 2) * 2]
                    pu, bpu = self.PF[(bk % 2) * 2 + 1], self.BPF[(bk % 2) * 2 + 1]
                    bk += 1
                    for a, (pp, bpp) in enumerate([(pg, bpg), (pu, bpu)]):
                        for kc in range(8):
                            S.op("pe", E("matmul",
                                pp[:, :], lhsT=wgu4[:, kc, a, m * 128:(m + 1) * 128], rhs=hT3[:, kc, tsl], start=(kc == 0), stop=(kc == 7)),
                                reads=[bwgu] + BhT[tg * 4:tg * 4 + 4], writes=[bpp])
                    si, bsi = SI[bk % 2]
                    S.op("act", E("activation", out=si, in_=pg[:, :], func=AF.Silu), reads=[bpg], writes=[bsi])
                    S.op("dve", E("tensor_tensor", out=ACT3[:, m, tsl], in0=si, in1=pu[:, :], op=ALU.mult),
                         reads=[bsi, bpu], writes=[BACT[tg]])
            for i in range(NT):
                for h2 in range(2):
                    po, bpo = self.PF[4 + h2], self.BPF[4 + h2]
                    for m in range(gm):
                        S.op("pe", E("matmul",
                            po[:, :], lhsT=ACT3[:, m, i * 128:(i + 1) * 128], rhs=wof3[:, m, h2 * 512:(h2 + 1) * 512],
                            start=(m == 0), stop=(m == gm - 1)), reads=[BACT[i // 4], bwof], writes=[bpo])
                    hs = slice(h2 * 512, (h2 + 1) * 512)
                    tmp, btmp = self.NRS[2 + h2], self.BNRS[2 + h2]
                    S.op("dve", E("tensor_tensor", out=tmp[:, :], in0=po[:, :], in1=self.ADA[:, 2, hs], op=ALU.mult),
                         reads=[bpo, self.BADA], writes=[btmp])
                    S.op("pool", E("tensor_tensor", out=self.X[:, i, hs], in0=self.X[:, i, hs], in1=tmp[:, :], op=ALU.add),
                         reads=[btmp, self.BX[i]], writes=[self.BX[i]])
        AR.release(mk)


def _consts(j):
    c = {}
    inv = (1.0 / (10000.0 ** (np.arange(0, 64, 2, dtype=np.float32) / np.float32(64)))).astype(np.float32)
    c["invf"] = np.broadcast_to(inv[None, :], (128, 32)).copy()
    n = np.arange(512)
    jb = np.arange(128)
    ovl = ((16 * n[:, None] < 64 * jb[None, :] + 64) & (16 * n[:, None] + 32 > 64 * jb[None, :])).astype(np.float32)
    ovl[511] = 0.0
    c["ovl"] = ovl.reshape(4, 128, 128).transpose(1, 0, 2).copy()
    key = np.arange(128)[:, None]
    q = np.arange(128)[None, :]
    tri = np.where(key > q, MASKV, 0.0).astype(np.float32)
    tri2 = np.where(key <= q, MASKV, 0.0).astype(np.float32)
    allm = np.full((128, 128), MASKV, np.float32)
    zero = np.zeros((128, 128), np.float32)
    c["diag"] = np.stack([zero if jj < j else (tri if jj == j else allm) for jj in range(4)], axis=1)
    win = []
    for s in range(8):
        rel = s - 4 - j
        if rel < -4 or rel > 0:
            win.append(allm)
        elif rel == -4:
            win.append(tri2)
        elif rel == 0:
            win.append(tri)
        else:
            win.append(zero)
    c["win"] = np.stack(win, axis=1)

    def cm(m):
        nl = np.arange(128)[:, None]
        return np.where(16 * nl + 31 - q <= 128 * m, 0.0, MASKV).astype(np.float32)
    c["cma"] = np.stack([cm(4 * r + j) for r in range(4)], axis=1)
    c["cmb"] = cm(16) if j == 0 else zero
    qq = np.arange(128)[:, None]
    cc = np.arange(256)[None, :]
    c["fwj"] = np.where(cc == 128 + 2 * j + (qq >= 64), 1.0e4, 0.0).astype(np.float32)
    c["cwj"] = np.where((cc - 128 < 2 * j + 1) | ((cc - 128 == 2 * j + 1) & (qq >= 64)), 1.0e30, -1.0e30).astype(np.float32)
    c64 = np.arange(64)[None, :]
    c["cwm"] = np.broadcast_to(np.where(c64 < 32 + j // 2, 1.0e30, -1.0e30), (128, 64)).astype(np.float32).copy()
    c["noj"] = np.broadcast_to(np.where(c64 < 32 + j // 2, 1.0, 0.0), (128, 64)).astype(np.float32).copy()
    return {"k_" + k: np.ascontiguousarray(v) for k, v in c.items()}


def _core_inputs(c, x, cvec, positions):
    b, j = c // 4, c % 4
    m = _consts(j)
    xt = x[b].reshape(64, 128, D)
    m["x_own"] = np.ascontiguousarray(xt[j::4])
    m["c_col"] = np.ascontiguousarray(cvec[b].reshape(8, 128).T)
    pt = positions[b].reshape(64, 128)
    m["k_pos_own"] = np.ascontiguousarray(pt[j::4].T.astype(np.int32))
    idx = np.minimum(16 * np.arange(512) + 31, SEQ - 1)
    m["k_pos_cmp"] = np.ascontiguousarray(positions[b][idx].reshape(4, 128).T.astype(np.int32))
    return m


_PROGS = {}


def _get_prog(mode, layers):
    key = (mode, tuple(layers))
    if key not in _PROGS:
        p = Prog(mode, layers)
        p.build()
        _PROGS[key] = p
    return _PROGS[key]


def kernel(**inputs):
    x = np.asarray(inputs["x"], np.float32)
    cvec = np.asarray(inputs["c"], np.float32)
    positions = np.asarray(inputs["positions"])
    W = {k: np.asarray(inputs[k], np.float32) for k in LAYER_W}
    base = [_core_inputs(c, x, cvec, positions) for c in range(8)]
    xs = [base[c]["x_own"] for c in range(8)]
    pa = _get_prog("A", [0])
    pb = _get_prog("B", [0])
    for l in range(DEPTH):
        wl = {k: np.ascontiguousarray(W[k][l]) for k in LAYER_W}
        in_a = []
        for c in range(8):
            m = dict(base[c])
            m.update(wl)
            m["x_own"] = xs[c]
            in_a.append(m)
        in_a = [{k: v for k, v in m.items() if k in pa.inputs} for m in in_a]
        ra = run_bass_kernel_spmd(pa.nc, in_a, core_ids=list(range(8))).results
        in_b = []
        for c in range(8):
            b0 = (c // 4) * 4
            m = dict(base[c])
            m.update(wl)
            m["x_own"] = xs[c]
            m["KT_all"] = np.stack([ra[b0 + r]["KT_own"] for r in range(4)], axis=0)
            m["V_all"] = np.stack([ra[b0 + r]["V_own"] for r in range(4)], axis=0)
            m["QA_d"] = ra[c]["QA_d"]
            m["QB_d"] = ra[c]["QB_d"]
            m["GA_d"] = ra[c]["GA_d"]
            in_b.append(m)
        in_b = [{k: v for k, v in m.items() if k in pb.inputs} for m in in_b]
        rb = run_bass_kernel_spmd(pb.nc, in_b, core_ids=list(range(8))).results
        xs = [rb[c]["x_out"] for c in range(8)]
    out = np.zeros((NBATCH, 64, 128, D), np.float32)
    for c in range(8):
        b, j = c // 4, c % 4
        out[b, j::4] = xs[c]
    return out.reshape(NBATCH, SEQ, D)


def _fused_inputs(b, x, cvec, positions, W):
    cs = [_consts(j) for j in range(4)]
    m = {}
    for k in cs[0]:
        if k in ("k_invf", "k_ovl"):
            m[k] = cs[0][k]
        else:
            m[k] = np.ascontiguousarray(np.stack([cs[j][k] for j in range(4)], 0))
    xt = x[b].reshape(64, 128, D)
    m["x_own"] = np.ascontiguousarray(np.stack([xt[j::4] for j in range(4)], 0))
    m["c_col"] = np.ascontiguousarray(cvec[b].reshape(8, 128).T)
    pt = positions[b].reshape(64, 128)
    m["k_pos_own"] = np.ascontiguousarray(np.stack([pt[j::4].T for j in range(4)], 0).astype(np.int32))
    idx = np.minimum(16 * np.arange(512) + 31, SEQ - 1)
    m["k_pos_cmp"] = np.ascontiguousarray(positions[b][idx].reshape(4, 128).T.astype(np.int32))
    m.update(W)
    return m


def kernel_fused(**inputs):
    x = np.asarray(inputs["x"], np.float32)
    cvec = np.asarray(inputs["c"], np.float32)
    positions = np.asarray(inputs["positions"])
    W = {k: np.ascontiguousarray(np.asarray(inputs[k], np.float32)) for k in LAYER_W}
    pf = _get_prog("F", list(range(DEPTH)))
    per_b = [_fused_inputs(b, x, cvec, positions, W) for b in range(NBATCH)]
    in_maps = [{k: v for k, v in per_b[c // 4].items() if k in pf.inputs} for c in range(8)]
    res = run_bass_kernel_spmd(pf.nc, in_maps, core_ids=list(range(8))).results
    out = np.zeros((NBATCH, 64, 128, D), np.float32)
    for b in range(NBATCH):
        xo = np.asarray(res[4 * b]["x_out"])
        for j in range(4):
            out[b, j::4] = xo[j]
    return out.reshape(NBATCH, SEQ, D)
```
